# Optimizing a Trainium2 kernel written in Bass

```python
import math
import jax, jax.numpy as jnp
from jax import lax
import numpy as np

D_MODEL = 2048
BATCH = 4
SEQ = 2048
DEPTH = 1

D_HEAD = 128
NSA_HEADS = 8
NSA_KV_GROUPS = 2
NSA_Q_PER_KV = NSA_HEADS // NSA_KV_GROUPS
SB_HEADS = 8
MIX_WIDTH = (NSA_HEADS + SB_HEADS) * D_HEAD
NSA_KV_WIDTH = NSA_KV_GROUPS * D_HEAD
IN_WIDTHS = (NSA_HEADS * D_HEAD,) + (NSA_KV_WIDTH,) * 6 + (NSA_HEADS * 3,) + (SB_HEADS * D_HEAD,) * 3
IN_COLS = sum(IN_WIDTHS)
CMP_LEN = 32
CMP_STRIDE = 16
SEL_LEN = 64
SEL_TOPK = 16
WINDOW = 512
Q_BLOCK = 128
T5_BUCKETS = 32
T5_MAX_DIST = 128
N_MEM = 256
MEM_HEADS = 4
MEM_HEAD_DIM = 128
PEER_N_KEYS = 128
PEER_N_EXPERTS = PEER_N_KEYS * PEER_N_KEYS
PEER_HEADS = 8
PEER_TOPK = 16
PEER_D_KEY = 256
PEER_TOK_BLOCK = 128

RMS_EPS = 1e-6
NEG_INF = -1e30
BIG = 1e9

kernel_name = 'hybrid_nsa_stickbreak_peer'


def rmsnorm(x, g):
    xf = x.astype(jnp.float32)
    y = xf * lax.rsqrt(jnp.mean(xf * xf, axis=-1, keepdims=True) + RMS_EPS)
    return (y * g.astype(jnp.float32)).astype(x.dtype)


def t5_bucket(dist):
    n = jnp.maximum(dist, 0)
    max_exact = T5_BUCKETS // 2
    nf = jnp.maximum(n, 1).astype(jnp.float32)
    large = max_exact + (jnp.log(nf / max_exact) / math.log(T5_MAX_DIST / max_exact)
                         * (T5_BUCKETS - max_exact)).astype(jnp.int32)
    large = jnp.minimum(large, T5_BUCKETS - 1)
    return jnp.where(n < max_exact, n, large)


def masked_softmax(logits, mask):
    logits = jnp.where(mask, logits.astype(jnp.float32), NEG_INF)
    m = jnp.max(logits, axis=-1, keepdims=True)
    e = jnp.where(mask, jnp.exp(logits - m), 0.0)
    return e / jnp.maximum(jnp.sum(e, axis=-1, keepdims=True), 1e-20)


def split_points():
    return [int(v) for v in np.cumsum(np.array(IN_WIDTHS))[:-1]]


def compress_blocks(kv, pos_emb, w1, w2):
    B, G, S, Dh = kv.shape
    n_cmp = (S - CMP_LEN) // CMP_STRIDE + 1
    idx = jnp.arange(n_cmp)[:, None] * CMP_STRIDE + jnp.arange(CMP_LEN)[None, :]
    blocks = kv[:, :, idx] + pos_emb
    flat = blocks.reshape(B, G, n_cmp, CMP_LEN * Dh)
    return jax.nn.gelu(flat @ w1) @ w2


def nsa_attention(q, k_cmp, v_cmp, k_slc, v_slc, k_win, v_win, gate_logits,
                  cmp_pos_k, cmp_w1_k, cmp_w2_k, cmp_pos_v, cmp_w1_v, cmp_w2_v, t5_table):
    B, S = q.shape[0], q.shape[1]
    G, R, Dh = NSA_KV_GROUPS, NSA_Q_PER_KV, D_HEAD
    scale = Dh ** -0.5
    q = q.reshape(B, S, G, R, Dh).transpose(0, 2, 3, 1, 4)

    def to_kv(a):
        return a.reshape(B, S, G, Dh).transpose(0, 2, 1, 3)

    k_cmp, v_cmp, k_slc, v_slc, k_win, v_win = (to_kv(a) for a in (k_cmp, v_cmp, k_slc, v_slc, k_win, v_win))
    t = jnp.arange(S)

    kc = compress_blocks(k_cmp, cmp_pos_k, cmp_w1_k, cmp_w2_k)
    vc = compress_blocks(v_cmp, cmp_pos_v, cmp_w1_v, cmp_w2_v)
    n_cmp = kc.shape[2]
    c_start = jnp.arange(n_cmp) * CMP_STRIDE
    dist_c = t[:, None] - (c_start + CMP_LEN - 1)[None, :]
    bias_c = t5_table[t5_bucket(dist_c)].transpose(2, 0, 1).reshape(G, R, S, n_cmp)
    logits_c = jnp.einsum('bgrsd,bgcd->bgrsc', q, kc).astype(jnp.float32) * scale + bias_c
    p_cmp = masked_softmax(logits_c, dist_c >= 0)
    o_cmp = jnp.einsum('bgrsc,bgcd->bgrsd', p_cmp.astype(vc.dtype), vc)

    n_sel = S // SEL_LEN
    j = jnp.arange(n_sel)
    sel_start = j * SEL_LEN
    overlap = jnp.clip(jnp.minimum(c_start[:, None] + CMP_LEN, sel_start[None, :] + SEL_LEN)
                       - jnp.maximum(c_start[:, None], sel_start[None, :]), 0, None).astype(jnp.float32) / CMP_LEN
    importance = jnp.einsum('bgrsc,cj->bgsj', p_cmp, overlap)
    cur = t // SEL_LEN
    forced = (j[None, :] == 0) | (j[None, :] == cur[:, None]) | (j[None, :] == cur[:, None] - 1)
    allowed = sel_start[None, :] <= t[:, None]
    score = jnp.where(forced, BIG, jnp.where(allowed, importance, -BIG))
    k_sel = min(SEL_TOPK, n_sel)
    _, sel_idx = lax.top_k(score, k_sel)

    ks_blocks = k_slc.reshape(B, G, n_sel, SEL_LEN, Dh)
    vs_blocks = v_slc.reshape(B, G, n_sel, SEL_LEN, Dh)
    k_win_pad = jnp.pad(k_win, ((0, 0), (0, 0), (WINDOW, 0), (0, 0)))
    v_win_pad = jnp.pad(v_win, ((0, 0), (0, 0), (WINDOW, 0), (0, 0)))
    table_gr = t5_table.T.reshape(G, R, T5_BUCKETS)
    b_ix = jnp.arange(B)[:, None, None, None]
    g_ix = jnp.arange(G)[None, :, None, None]
    n_kw = WINDOW + Q_BLOCK
    n_tok_sel = k_sel * SEL_LEN

    def query_block(i):
        qs = i * Q_BLOCK
        qb = lax.dynamic_slice_in_dim(q, qs, Q_BLOCK, axis=3)
        tq = qs + jnp.arange(Q_BLOCK)
        idx = lax.dynamic_slice_in_dim(sel_idx, qs, Q_BLOCK, axis=2)
        ksel = ks_blocks[b_ix, g_ix, idx].reshape(B, G, Q_BLOCK, n_tok_sel, Dh)
        vsel = vs_blocks[b_ix, g_ix, idx].reshape(B, G, Q_BLOCK, n_tok_sel, Dh)
        pos = (idx[..., None] * SEL_LEN + jnp.arange(SEL_LEN)).reshape(B, G, Q_BLOCK, n_tok_sel)
        dist = tq[:, None] - pos
        bias = table_gr[g_ix, :, t5_bucket(dist)].transpose(0, 1, 4, 2, 3)
        logits = jnp.einsum('bgrqd,bgqnd->bgrqn', qb, ksel).astype(jnp.float32) * scale + bias
        p = masked_softmax(logits, (dist >= 0)[:, :, None])
        o_slc = jnp.einsum('bgrqn,bgqnd->bgrqd', p.astype(vsel.dtype), vsel)
        kw = lax.dynamic_slice_in_dim(k_win_pad, qs, n_kw, axis=2)
        vw = lax.dynamic_slice_in_dim(v_win_pad, qs, n_kw, axis=2)
        posw = qs - WINDOW + jnp.arange(n_kw)
        distw = tq[:, None] - posw[None, :]
        maskw = (posw[None, :] >= 0) & (distw >= 0) & (distw < WINDOW)
        biasw = t5_table[t5_bucket(distw)].transpose(2, 0, 1).reshape(G, R, Q_BLOCK, n_kw)
        logits_w = jnp.einsum('bgrqd,bgnd->bgrqn', qb, kw).astype(jnp.float32) * scale + biasw
        pw = masked_softmax(logits_w, maskw)
        o_win = jnp.einsum('bgrqn,bgnd->bgrqd', pw.astype(vw.dtype), vw)
        return o_slc, o_win

    o_slc, o_win = lax.map(query_block, jnp.arange(S // Q_BLOCK))

    def unblock(o):
        return o.transpose(1, 2, 3, 0, 4, 5).reshape(B, G, R, S, Dh)

    gates = jax.nn.sigmoid(gate_logits.astype(jnp.float32)).reshape(B, S, G, R, 3)
    gates = gates.transpose(0, 2, 3, 1, 4).astype(q.dtype)
    o = gates[..., 0:1] * o_cmp + gates[..., 1:2] * unblock(o_slc) + gates[..., 2:3] * unblock(o_win)
    return o.transpose(0, 3, 1, 2, 4).reshape(B, S, NSA_HEADS * Dh)


def stick_breaking_attention(q, k, v):
    B, S = q.shape[0], q.shape[1]
    scale = D_HEAD ** -0.5
    q, k, v = (a.reshape(B, S, SB_HEADS, D_HEAD).transpose(0, 2, 1, 3) for a in (q, k, v))
    s_pos = jnp.arange(S)

    def query_block(i):
        qs = i * Q_BLOCK
        qb = lax.dynamic_slice_in_dim(q, qs, Q_BLOCK, axis=2)
        tq = qs + jnp.arange(Q_BLOCK)
        z = jnp.einsum('bhqd,bhsd->bhqs', qb, k).astype(jnp.float32) * scale
        mask = s_pos[None, :] < tq[:, None]
        log_fail = jnp.where(mask, jax.nn.log_sigmoid(-z), 0.0)
        after = lax.cumsum(log_fail, axis=3, reverse=True) - log_fail
        w = jnp.where(mask, jnp.exp(jax.nn.log_sigmoid(z) + after), 0.0)
        return jnp.einsum('bhqs,bhsd->bhqd', w.astype(v.dtype), v)

    o = lax.map(query_block, jnp.arange(S // Q_BLOCK))
    return o.transpose(1, 0, 3, 2, 4).reshape(B, S, SB_HEADS * D_HEAD)


def memory_attention(hn, mem_n, w_q, w_k, w_v, w_o):
    B, S = hn.shape[0], hn.shape[1]
    M = mem_n.shape[1]
    q = (hn @ w_q).reshape(B, S, MEM_HEADS, MEM_HEAD_DIM)
    k = (mem_n @ w_k).reshape(B, M, MEM_HEADS, MEM_HEAD_DIM)
    v = (mem_n @ w_v).reshape(B, M, MEM_HEADS, MEM_HEAD_DIM)
    logits = jnp.einsum('bshd,bmhd->bhsm', q, k).astype(jnp.float32) * (MEM_HEAD_DIM ** -0.5)
    p = jax.nn.softmax(logits, axis=-1)
    o = jnp.einsum('bhsm,bmhd->bshd', p.astype(v.dtype), v).reshape(B, S, MEM_HEADS * MEM_HEAD_DIM)
    return o @ w_o


def peer_ffn(hn, w_pq, sub_keys, u_tab, v_tab):
    B, S, D = hn.shape
    T = B * S
    K = PEER_TOPK
    xt = hn.reshape(T, D)
    q = (xt @ w_pq).reshape(T, PEER_HEADS, 2, PEER_D_KEY // 2)
    s = jnp.einsum('thpd,hpkd->thpk', q, sub_keys).astype(jnp.float32)
    top_s, top_i = lax.top_k(s, K)
    cand = top_s[:, :, 0, :, None] + top_s[:, :, 1, None, :]
    best_s, best_c = lax.top_k(cand.reshape(T, PEER_HEADS, K * K), K)
    ia = jnp.take_along_axis(top_i[:, :, 0], best_c // K, axis=-1)
    ib = jnp.take_along_axis(top_i[:, :, 1], best_c % K, axis=-1)
    expert = ia * PEER_N_KEYS + ib
    gate = jax.nn.softmax(best_s, axis=-1)
    nb = T // PEER_TOK_BLOCK
    E = PEER_HEADS * K
    xs = (xt.reshape(nb, PEER_TOK_BLOCK, D), expert.reshape(nb, PEER_TOK_BLOCK, E),
          gate.reshape(nb, PEER_TOK_BLOCK, E))

    def token_block(args):
        xb, eb, gb = args
        a = jnp.einsum('td,ted->te', xb, u_tab[eb])
        hidden = gb.astype(xb.dtype) * jax.nn.gelu(a)
        return jnp.einsum('te,ted->td', hidden, v_tab[eb])

    out = lax.map(token_block, xs)
    return out.reshape(B, S, D)


def hybrid_layer(h, mem, t5_table, norm_mix, w_in, cmp_pos_k, cmp_w1_k, cmp_w2_k,
                 cmp_pos_v, cmp_w1_v, cmp_w2_v, w_out, norm_mem_q, norm_mem_kv,
                 w_mem_q, w_mem_k, w_mem_v, w_mem_o, norm_ffn, peer_w_q, peer_sub_keys,
                 peer_u, peer_v):
    hn = rmsnorm(h, norm_mix)
    proj = hn @ w_in
    (q_nsa, k_cmp, v_cmp, k_slc, v_slc, k_win, v_win, gate_logits,
     q_sb, k_sb, v_sb) = jnp.split(proj, split_points(), axis=-1)
    o_nsa = nsa_attention(q_nsa, k_cmp, v_cmp, k_slc, v_slc, k_win, v_win, gate_logits,
                          cmp_pos_k, cmp_w1_k, cmp_w2_k, cmp_pos_v, cmp_w1_v, cmp_w2_v, t5_table)
    o_sb = stick_breaking_attention(q_sb, k_sb, v_sb)
    h = h + jnp.concatenate([o_nsa, o_sb], axis=-1) @ w_out
    h = h + memory_attention(rmsnorm(h, norm_mem_q), rmsnorm(mem, norm_mem_kv),
                             w_mem_q, w_mem_k, w_mem_v, w_mem_o)
    h = h + peer_ffn(rmsnorm(h, norm_ffn), peer_w_q, peer_sub_keys, peer_u, peer_v)
    return h


def setup_inputs(seed: int = 0) -> dict:
    key = jax.random.key(seed)
    ks = jax.random.split(key, 24)

    def nrm(k, shape, scale):
        return jax.random.normal(k, shape, jnp.float32) * scale

    def gain(k, shape):
        return 1.0 + 0.02 * jax.random.normal(k, shape, jnp.float32)

    L = DEPTH
    return {
        'x': nrm(ks[0], (BATCH, SEQ, D_MODEL), 1.0),
        'mem': nrm(ks[1], (BATCH, N_MEM, D_MODEL), 1.0),
        't5_table': nrm(ks[2], (T5_BUCKETS, NSA_HEADS), 0.2),
        'norm_mix': gain(ks[3], (L, D_MODEL)),
        'w_in': nrm(ks[4], (L, D_MODEL, IN_COLS), D_MODEL ** -0.5),
        'cmp_pos_k': nrm(ks[5], (L, CMP_LEN, D_HEAD), 0.02),
        'cmp_w1_k': nrm(ks[6], (L, CMP_LEN * D_HEAD, D_HEAD), (CMP_LEN * D_HEAD) ** -0.5),
        'cmp_w2_k': nrm(ks[7], (L, D_HEAD, D_HEAD), D_HEAD ** -0.5),
        'cmp_pos_v': nrm(ks[8], (L, CMP_LEN, D_HEAD), 0.02),
        'cmp_w1_v': nrm(ks[9], (L, CMP_LEN * D_HEAD, D_HEAD), (CMP_LEN * D_HEAD) ** -0.5),
        'cmp_w2_v': nrm(ks[10], (L, D_HEAD, D_HEAD), D_HEAD ** -0.5),
        'w_out': nrm(ks[11], (L, MIX_WIDTH, D_MODEL), MIX_WIDTH ** -0.5),
        'norm_mem_q': gain(ks[12], (L, D_MODEL)),
        'norm_mem_kv': gain(ks[13], (L, D_MODEL)),
        'w_mem_q': nrm(ks[14], (L, D_MODEL, MEM_HEADS * MEM_HEAD_DIM), D_MODEL ** -0.5),
        'w_mem_k': nrm(ks[15], (L, D_MODEL, MEM_HEADS * MEM_HEAD_DIM), D_MODEL ** -0.5),
        'w_mem_v': nrm(ks[16], (L, D_MODEL, MEM_HEADS * MEM_HEAD_DIM), D_MODEL ** -0.5),
        'w_mem_o': nrm(ks[17], (L, MEM_HEADS * MEM_HEAD_DIM, D_MODEL), (MEM_HEADS * MEM_HEAD_DIM) ** -0.5),
        'norm_ffn': gain(ks[18], (L, D_MODEL)),
        'peer_w_q': nrm(ks[19], (L, D_MODEL, PEER_HEADS * PEER_D_KEY), D_MODEL ** -0.5),
        'peer_sub_keys': nrm(ks[20], (L, PEER_HEADS, 2, PEER_N_KEYS, PEER_D_KEY // 2), (PEER_D_KEY // 2) ** -0.5),
        'peer_u': nrm(ks[21], (L, PEER_N_EXPERTS, D_MODEL), D_MODEL ** -0.5),
        'peer_v': nrm(ks[22], (L, PEER_N_EXPERTS, D_MODEL), (PEER_HEADS * PEER_TOPK) ** -0.5),
        'norm_final': gain(ks[23], (D_MODEL,)),
    }


def reference(x, mem, t5_table, norm_mix, w_in, cmp_pos_k, cmp_w1_k, cmp_w2_k,
              cmp_pos_v, cmp_w1_v, cmp_w2_v, w_out, norm_mem_q, norm_mem_kv,
              w_mem_q, w_mem_k, w_mem_v, w_mem_o, norm_ffn, peer_w_q, peer_sub_keys,
              peer_u, peer_v, norm_final):
    h = x
    for l in range(DEPTH):
        h = hybrid_layer(h, mem, t5_table, norm_mix[l], w_in[l], cmp_pos_k[l], cmp_w1_k[l],
                         cmp_w2_k[l], cmp_pos_v[l], cmp_w1_v[l], cmp_w2_v[l], w_out[l],
                         norm_mem_q[l], norm_mem_kv[l], w_mem_q[l], w_mem_k[l], w_mem_v[l],
                         w_mem_o[l], norm_ffn[l], peer_w_q[l], peer_sub_keys[l],
                         peer_u[l], peer_v[l])
    return rmsnorm(h, norm_final)
```

```python
import math
from contextlib import ExitStack

import numpy as np
import ml_dtypes

import concourse.bass as bass
import concourse.mybir as mybir
from concourse.bass_utils import run_bass_kernel_spmd

F32 = mybir.dt.float32
BF16 = mybir.dt.bfloat16
U32 = mybir.dt.uint32
AF = mybir.ActivationFunctionType
ALU = mybir.AluOpType
AX = mybir.AxisListType

D = 2048
S = 2048
NOWN = 1024
DH = 128
SCALE = DH ** -0.5
EPS = 1e-6
NEG = -30000.0
IN_COLS = 5656
C_QN, C_KC, C_VC, C_KS, C_VS, C_KW, C_VW, C_G, C_QS, C_KSB, C_VSB = 0, 1024, 1280, 1536, 1792, 2048, 2304, 2560, 2584, 3608, 4632
LG = 2560
PT = 2720
PTC = LG + 16 * 128 + 64


class Buf:
    __slots__ = ("w", "r", "name")

    def __init__(self, name=""):
        self.w = None
        self.r = {}
        self.name = name


class Sched:
    def __init__(self, nc, es, n_dma=56):
        self.nc = nc
        self.E = dict(pe=nc.tensor, act=nc.scalar, dve=nc.vector, pool=nc.gpsimd, sp=nc.sync)
        self.sem = {k: es.enter_context(nc.semaphore(f"sem_{k}")) for k in self.E}
        self.cnt = {k: 0 for k in self.E}
        self.seen = {k: {} for k in self.E}
        self.slots = [[es.enter_context(nc.semaphore(f"dsem{i}")), 0, None] for i in range(n_dma)]
        self.rr = 0
        self.cslots = [[es.enter_context(nc.semaphore(f"csem{i}")), 0, None] for i in range(8)]
        self.crr = 0
        self.n_wait = 0

    def _wait(self, eng, deps):
        need = {}
        for ev in deps:
            if ev is None:
                continue
            key, sem, val = ev
            if key == eng and eng == "pe":
                continue
            if need.get(key, (None, 0))[1] < val:
                need[key] = (sem, val)
        seen = self.seen[eng]
        for key, (sem, val) in need.items():
            if seen.get(key, 0) >= val:
                continue
            self.E[eng].wait_ge(sem, val)
            self.n_wait += 1
            seen[key] = val

    @staticmethod
    def _deps(reads, writes):
        deps = []
        for b in reads:
            deps.append(b.w)
        for b in writes:
            deps.append(b.w)
            deps.extend(b.r.values())
        return deps

    @staticmethod
    def _mark(ev, reads, writes):
        for b in reads:
            b.r[ev[0]] = ev
        for b in writes:
            b.w = ev
            b.r = {}

    def op(self, eng, fn, reads=(), writes=()):
        self._wait(eng, self._deps(reads, writes))
        inst = fn(self.E[eng])
        self.cnt[eng] += 1
        ev = (eng, self.sem[eng], self.cnt[eng])
        inst.then_inc(self.sem[eng], 1)
        self._mark(ev, reads, writes)
        return ev

    def dma(self, q, fn, reads=(), writes=(), conv=False):
        if conv:
            slot = self.cslots[self.crr]
            key = f"c{self.crr}"
            self.crr = (self.crr + 1) % len(self.cslots)
        else:
            slot = self.slots[self.rr]
            key = f"d{self.rr}"
            self.rr = (self.rr + 1) % len(self.slots)
        deps = self._deps(reads, writes)
        deps.append(slot[2])
        self._wait(q, deps)
        inst = fn(self.E[q])
        slot[1] += 16
        ev = (key, slot[0], slot[1])
        inst.then_inc(slot[0], 16)
        slot[2] = ev
        self._mark(ev, reads, writes)
        return ev

    def barrier(self):
        evs = []
        for k in self.E:
            if self.cnt[k]:
                evs.append((k, self.sem[k], self.cnt[k]))
        for i, s in enumerate(self.slots):
            if s[2] is not None:
                evs.append(s[2])
        for k in self.E:
            self._wait(k, [e for e in evs if e[0] != k or k != "pe"])

    def finish(self, evs):
        self._wait("sp", evs)


class T:
    def __init__(self, t, nparts=1, name=""):
        self.t = t
        self.b = [Buf(f"{name}{i}") for i in range(nparts)]

    def __getitem__(self, k):
        return self.t[k]


def t5_bucket_np(dist):
    n = np.maximum(dist, 0)
    nf = np.maximum(n, 1).astype(np.float32)
    large = 16 + (np.log(nf / np.float32(16)) / np.float32(math.log(128 / 16)) * np.float32(16)).astype(np.int32)
    large = np.minimum(large, 31)
    return np.where(n < 16, n, large)


def host_consts(par):
    q = [par, 3 - par]
    qoff = [512 * q[0], 512 * q[1]]
    c = {}
    i = np.arange(128)[:, None]
    j = np.arange(512)[None, :]
    sbm = np.zeros((24, 128, 512), np.float32)
    u = 0
    for ci in range(2):
        for kb in range(8 if ci == 0 else 16):
            sbm[u] = ((128 * kb + i) < (qoff[ci] + j)).astype(np.float32)
            u += 1
    c["sbmask"] = sbm.astype(ml_dtypes.bfloat16)
    oh = np.zeros((4, 33, LG), np.float32)
    m = np.arange(LG)
    for ci in range(2):
        n = m - 2047 + qoff[ci]
        bk = t5_bucket_np(n)
        for ty in range(2):
            valid = (n >= 0) & (n < 512) if ty == 0 else (n >= 0)
            o = oh[ci * 2 + ty]
            o[bk[valid], m[valid]] = 1.0
            o[32, m[~valid]] = 1.0
    c["oh"] = oh
    t_abs = np.concatenate([qoff[0] + np.arange(512), qoff[1] + np.arange(512)])[:, None]
    jj = np.arange(32)[None, :]
    cur = t_abs // 64
    forced = (jj == 0) | (jj == cur) | (jj == cur - 1)
    allowed = (jj * 64) <= t_abs
    selmul = (allowed & ~forced).astype(np.float32)
    seladd = np.where(forced, 1e9, np.where(allowed, 0.0, -1e9)).astype(np.float32)
    c["selmul"] = selmul
    c["seladd"] = seladd
    return c


def shared_consts():
    c = {}
    k = np.arange(128)[:, None]
    mm = np.arange(128)[None, :]
    cm = np.zeros((128, 5 * 128), np.float32)
    cm[:, 0:128] = (k == mm)
    cm[:, 128:256] = (k >= mm)
    cm[:, 256:384] = (k < mm)
    cm[:, 384:512] = 1.0
    c["cmat"] = cm
    es_ = np.zeros((32, 16 * 128), np.float32)
    for kb in range(16):
        for s in range(128):
            es_[2 * kb + (1 if s >= 64 else 0), kb * 128 + s] = 1.0
    c["esel"] = es_
    cs = np.arange(127) * 16
    ss = np.arange(32) * 64
    ov = np.clip(np.minimum(cs[:, None] + 32, ss[None, :] + 64) - np.maximum(cs[:, None], ss[None, :]), 0, None).astype(np.float32) / 32
    ovp = np.zeros((128, 32), np.float32)
    ovp[:127] = ov
    c["overlap"] = ovp
    c255 = np.zeros((128, 255), np.float32)
    c255[:, 127] = 1.0
    c["c255"] = c255
    c["iota16"] = np.tile((np.arange(2048) % 16).astype(np.float32)[None, :], (128, 1))
    return c


def build_program(stop="all", dbg=None):
    nc = bass.Bass("TRN2", target_bir_lowering=False)
    es = ExitStack()

    def din(name, shape, dt=F32):
        return nc.dram_tensor(name, list(shape), dt, kind="ExternalInput").ap()

    def dscr(name, shape, dt=BF16):
        return nc.dram_tensor(name, list(shape), dt, kind="Internal")

    xa = din("xa", [S, D])
    xo = din("xo", [NOWN, D])
    memb = din("memb", [256, D])
    t5 = din("t5", [32, 8])
    g_mix = din("g_mix", [1, D]); g_mq = din("g_mq", [1, D]); g_mkv = din("g_mkv", [1, D])
    g_ffn = din("g_ffn", [1, D]); g_fin = din("g_fin", [1, D])
    w_in = din("w_in", [D, IN_COLS])
    posT = din("posT", [2, 128, 32])
    w1 = din("w1", [2, 4096, 128]); w2 = din("w2", [2, 128, 128])
    w_out = din("w_out", [D, D])
    wmq = din("wmq", [D, 512]); wmk = din("wmk", [D, 512]); wmv = din("wmv", [D, 512]); wmo = din("wmo", [512, D])
    wpq = din("wpq", [D, D])
    skT = din("skT", [16, 128, 128])
    pu = din("pu", [16384, D]); pv = din("pv", [16384, D])
    cmat = din("cmat", [128, 640]); esel_d = din("esel", [32, 2048]); ovl_d = din("overlap", [128, 32])
    c255_d = din("c255", [128, 255]); iota16_d = din("iota16", [128, 2048])
    sbmask_d = din("sbmask", [24, 128, 512], BF16)
    oh_d = din("oh", [4, 33, LG])
    selmul_d = din("selmul", [NOWN, 32]); seladd_d = din("seladd", [NOWN, 32])
    y = nc.dram_tensor("y", [NOWN, D], F32, kind="ExternalOutput").ap()
    dbg_t = None
    if dbg is not None:
        dbg_t = nc.dram_tensor("dbg", list(dbg), F32, kind="ExternalOutput").ap()

    QNd = dscr("QNd", [1024, NOWN]).ap(); QSd = dscr("QSd", [1024, NOWN]).ap()
    KNd = dscr("KNd", [1024, S]).ap(); KSd = dscr("KSd", [1024, S]).ap()
    VNd = dscr("VNd", [S, 512]).ap(); VSd = dscr("VSd", [S, 1024]).ap()
    Gd = dscr("Gd", [4, 8, LG])
    Bw = [[dscr(f"Bw{ci}_{h}", [128 * (PT + 1)]) for h in range(8)] for ci in range(2)]
    Bs = [[dscr(f"Bs{ci}_{h}", [128 * (PT + 1)]) for h in range(8)] for ci in range(2)]
    Bc = [[dscr(f"Bc{ci}_{h}", [128 * (PTC + 16)]) for h in range(8)] for ci in range(2)]
    pu16 = dscr("pu16", [16384, D]).ap(); pv16 = dscr("pv16", [16384, D]).ap()
    tabB = [Buf(f"tab{i}") for i in range(1024)]
    scrB = Buf("scr")
    qkvB = {k: Buf(k) for k in ("QN", "QS", "KN", "KS", "VN", "VS")}

    sch = Sched(nc, es)
    op = sch.op
    dma = sch.dma
    conv_state = [0]
    NCH = 1024

    def conv_step(n=1, dep=None):
        for _ in range(n):
            i = conv_state[0]
            if i >= NCH:
                return
            conv_state[0] += 1
            src, dst = (pu, pu16) if i < NCH // 2 else (pv, pv16)
            r0 = (i % (NCH // 2)) * 32
            dma("pool", lambda e: e.dma_start(out=dst[r0:r0 + 32, :], in_=src[r0:r0 + 32, :]), reads=(dep or []), writes=[tabB[i]], conv=True)

    def sb(name, shape, dt, nparts=1, stack=None):
        t = (stack or es).enter_context(nc.sbuf_tensor("s_" + name, list(shape), dt))
        return T(t, nparts, name)

    PA = es.enter_context(nc.psum_tensor("PA", [128, 2048], F32))
    PB = es.enter_context(nc.psum_tensor("PB", [128, 2048], F32))
    bankB = [Buf(f"bank{i}") for i in range(8)]

    def bank(i):
        t = PA if i < 4 else PB
        k = i % 4
        return t[:, k * 512:(k + 1) * 512]

    cm_f = sb("cm_f", [128, 640], F32)
    cm_b = sb("cm_b", [128, 512], BF16)
    dma("sp", lambda e: e.dma_start(out=cm_f[:], in_=cmat[:, :]), writes=cm_f.b)
    op("dve", lambda e: e.tensor_copy(out=cm_b[:], in_=cm_f[:, 0:512]), reads=cm_f.b, writes=cm_b.b)
    identf = cm_f[:, 0:128]
    identb = cm_b[:, 0:128]
    trib = cm_b[:, 128:256]
    ntrib = cm_b[:, 256:384]
    onesb = cm_b[:, 384:512]
    CB = cm_f.b + cm_b.b

    evac_rr = [0]

    def evac(out_ap, in_ap, reads, writes, scale=None):
        evac_rr[0] ^= 1
        if evac_rr[0] or scale is not None:
            if scale is None:
                return op("act", lambda e: e.activation(out=out_ap, in_=in_ap, func=AF.Copy), reads=reads, writes=writes)
            return op("act", lambda e: e.activation(out=out_ap, in_=in_ap, func=AF.Copy, scale=scale), reads=reads, writes=writes)
        return op("dve", lambda e: e.tensor_copy(out=out_ap, in_=in_ap), reads=reads, writes=writes)

    scr_parts = []
    def init_scratch(stk):
        tblx = sb("tblx", [33, 8], F32, 1, stk)
        ohs2 = [sb(f"ohs{i}", [33, 512], F32, 1, stk) for i in range(2)]
        Gs2 = [sb(f"Gs{i}", [8, 512], BF16, 1, stk) for i in range(2)]
        GdB = [Buf(f"Gd{k}") for k in range(4)]
        dma("sp", lambda e: e.dma_start(out=tblx[0:32, :], in_=t5[:, :]), writes=tblx.b)
        op("act", lambda e: e.activation(out=tblx[0:32, :], in_=tblx[0:32, :], func=AF.Copy, scale=1.0 / SCALE), reads=tblx.b, writes=tblx.b)
        op("dve", lambda e: e.memset(tblx[32:33, :], NEG), writes=tblx.b)
        scr_all = [(k, n) for k in range(4) for n in range(5)]
        scr_pos = [0]

        def scr_load(i):
            if i < len(scr_all):
                k, n = scr_all[i]
                ohs = ohs2[i % 2]
                dma("sp", lambda e: e.dma_start(out=ohs[:], in_=oh_d[k][:, n * 512:(n + 1) * 512]), writes=ohs.b)

        scr_load(0)

        def scratch_step():
            i = scr_pos[0]
            if i >= len(scr_all):
                return
            scr_pos[0] += 1
            scr_load(i + 1)
            k, n = scr_all[i]
            ci, ty = k // 2, k % 2
            ohs = ohs2[i % 2]; Gs = Gs2[i % 2]
            op("pe", lambda e: e.matmul(bank(7)[0:8, :], tblx[0:33, :], ohs[0:33, :], start=True, stop=True), reads=tblx.b + ohs.b, writes=[bankB[7]])
            op("act", lambda e: e.activation(out=Gs[0:8, :], in_=bank(7)[0:8, :], func=AF.Copy), reads=[bankB[7]], writes=Gs.b)
            dma("act", lambda e: e.dma_start(out=Gd.ap()[k][:, n * 512:(n + 1) * 512], in_=Gs[:]), reads=Gs.b, writes=[GdB[k]])
            if n == 4:
                for h in range(8):
                    src = bass.AP(Gd, (k * 8 + h) * LG, [[0, 128], [1, LG]])
                    B = (Bw if ty == 0 else Bs)[ci][h]
                    bB = Buf("scrh")
                    scr_parts.append(bB)
                    dma("pool", lambda e: e.dma_start(out=bass.AP(B, 0, [[PT + 1, 128], [1, LG]]), in_=src), reads=[GdB[k]], writes=[bB])
                    if ty == 1:
                        bB2 = Buf("scrc")
                        scr_parts.append(bB2)
                        dma("pool", lambda e: e.dma_start(out=bass.AP(Bc[ci][h], 0, [[PTC + 16, 128], [1, LG]]), in_=src), reads=[GdB[k]], writes=[bB2])
        return scratch_step

    glT = sb("glT", [24, NOWN], F32)
    hres = sb("hres", [128, 8, D], F32, 8)
    ph1 = ExitStack()
    xaT = sb("xaT", [128, 16, S], BF16, 1, ph1)
    xoT = sb("xoT", [128, 16, NOWN], BF16, 1, ph1)

    def load_gain(name, g_ap, stack):
        gt = sb(name, [128, D], F32, 1, stack)
        src = bass.AP(g_ap.tensor, 0, [[0, 128], [1, D]])
        dma("sp", lambda e: e.dma_start(out=gt[:], in_=src), writes=gt.b)
        return gt

    def rms_tile(xt, gt, xn_out, stack_tiles):
        junk, ssq, rs = stack_tiles
        op("act", lambda e: e.activation(out=junk[:], in_=xt[:], func=AF.Square, accum_out=ssq[:, 0:1]),
           reads=xt.b, writes=junk.b + ssq.b)
        op("dve", lambda e: e.tensor_scalar(out=rs[:, 0:1], in0=ssq[:, 0:1], scalar1=1.0 / D, scalar2=EPS, op0=ALU.mult, op1=ALU.add),
           reads=ssq.b, writes=rs.b)
        op("act", lambda e: e.activation(out=rs[:, 0:1], in_=rs[:, 0:1], func=AF.Sqrt), reads=rs.b, writes=rs.b)
        op("dve", lambda e: e.reciprocal(out=rs[:, 0:1], in_=rs[:, 0:1]), reads=rs.b, writes=rs.b)
        op("dve", lambda e: e.scalar_tensor_tensor(out=xn_out[:], in0=xt[:], scalar=rs[:, 0:1], in1=gt[:], op0=ALU.mult, op1=ALU.mult),
           reads=xt.b + rs.b + gt.b, writes=xn_out.b)

    PAb = PA[:].bitcast(BF16)
    PBb = PB[:].bitcast(BF16)

    def transpose_tile(xn, dstT, col0, pview, pbanks):
        for c in range(16):
            op("pe", lambda e, c=c: e.transpose(out=pview[:, c * 128:(c + 1) * 128], in_=xn[:, c * 128:(c + 1) * 128], identity=identb),
               reads=xn.b + CB, writes=pbanks)
        src = pview[:, 0:2048].rearrange("p (c t) -> p c t", c=16)
        op("act", lambda e: e.activation(out=dstT[:, 0:8, col0:col0 + 128], in_=src[:, 0:8, :], func=AF.Copy), reads=pbanks, writes=dstT.b)
        op("dve", lambda e: e.tensor_copy(out=dstT[:, 8:16, col0:col0 + 128], in_=src[:, 8:16, :]), reads=pbanks, writes=dstT.b)

    with ExitStack() as st:
        gmix = load_gain("gmix", g_mix, st)
        scratch_step = init_scratch(st)
        xts = [sb(f"xt{i}", [128, D], F32, 1, st) for i in range(2)]
        xns = [sb(f"xn{i}", [128, D], BF16, 1, st) for i in range(2)]
        ssqs = [sb(f"ssq{i}", [128, 1], F32, 1, st) for i in range(2)]
        rss = [sb(f"rs{i}", [128, 1], F32, 1, st) for i in range(2)]
        n = 0
        for src, dstT, ntile in ((xa, xaT, 16), (xo, xoT, 8)):
            for tt in range(ntile):
                xt = xts[n % 2]; xn = xns[n % 2]
                dma("sp", lambda e, xt=xt, src=src, tt=tt: e.dma_start(out=xt[:], in_=src[tt * 128:(tt + 1) * 128, :]), writes=xt.b)
                rms_tile(xt, gmix, xn, (xn, ssqs[n % 2], rss[n % 2]))
                if n % 2 == 0:
                    transpose_tile(xn, dstT, tt * 128, PAb, bankB[0:2])
                else:
                    transpose_tile(xn, dstT, tt * 128, PBb, bankB[4:6])
                n += 1
                scratch_step()
        sch.barrier()

    w_in_v = w_in.rearrange("(c p) n -> p c n", p=128)
    with ExitStack() as st:
        wb = [sb(f"wb{i}", [128, 16, 128], BF16, 1, st) for i in range(3)]
        wbig = [sb(f"wbig{i}", [128, 16, 512], BF16, 1, st) for i in range(1)]
        stg = [sb(f"stg{i}", [128, 512], BF16, 1, st) for i in range(4)]
        nst = [0]
        nbk = [0]

        def next_bank():
            nbk[0] = (nbk[0] + 1) % 8
            return nbk[0]

        def fm_block(col0, srcT, ntc, dst, row0, key, nw):
            w = wb[nw % 3]
            dma("pool", lambda e: e.dma_start(out=w[:], in_=w_in_v[:, :, col0:col0 + 128]), writes=w.b)
            for tc in range(ntc):
                bi = next_bank()
                for c in range(16):
                    op("pe", lambda e, c=c, bi=bi, tc=tc: e.matmul(bank(bi), w[:, c, :], srcT[:, c, tc * 512:(tc + 1) * 512], start=(c == 0), stop=(c == 15)),
                       reads=w.b + srcT.b, writes=[bankB[bi]])
                s_ = stg[nst[0] % 4]; nst[0] += 1
                evac(s_[:], bank(bi), [bankB[bi]], s_.b)
                dma("sp", lambda e, s_=s_, tc=tc: e.dma_start(out=dst[row0:row0 + 128, tc * 512:(tc + 1) * 512], in_=s_[:]), reads=s_.b, writes=[qkvB[key]])


        nw = 0
        for ty, cbase in enumerate((C_KC, C_VC, C_KS, C_KW)):
            for g in range(2):
                fm_block(cbase + 128 * g, xaT, 4, KNd, (ty * 2 + g) * 128, "KN", nw); nw += 1
        for h in range(8):
            fm_block(C_KSB + 128 * h, xaT, 4, KSd, h * 128, "KS", nw); nw += 1
        for h in range(8):
            fm_block(C_QN + 128 * h, xoT, 2, QNd, h * 128, "QN", nw); nw += 1
        for h in range(8):
            fm_block(C_QS + 128 * h, xoT, 2, QSd, h * 128, "QS", nw); nw += 1
        wg = wb[nw % 3]; nw += 1
        dma("pool", lambda e: e.dma_start(out=wg[:, :, 0:24], in_=w_in_v[:, :, C_G:C_G + 24]), writes=wg.b)
        for tc in range(2):
            bi = next_bank()
            for c in range(16):
                op("pe", lambda e, c=c, bi=bi, tc=tc: e.matmul(bank(bi)[0:24, :], wg[:, c, 0:24], xoT[:, c, tc * 512:(tc + 1) * 512], start=(c == 0), stop=(c == 15)),
                   reads=wg.b + xoT.b, writes=[bankB[bi]])
            op("act", lambda e, bi=bi, tc=tc: e.activation(out=glT[0:24, tc * 512:(tc + 1) * 512], in_=bank(bi)[0:24, :], func=AF.Exp, scale=-1.0),
               reads=[bankB[bi]], writes=glT.b)
        op("act", lambda e: e.activation(out=glT[0:24, :], in_=glT[0:24, :], func=AF.Ln, bias=1.0), reads=glT.b, writes=glT.b)
        passes = [(VNd, "VN", [(C_VS, 256), (C_VW, 256)], 0), (VSd, "VS", [(C_VSB, 512)], 0), (VSd, "VS", [(C_VSB + 512, 512)], 512)]
        for pi, (dst, key, segs, dcol) in enumerate(passes):
            w = wbig[0]
            o = 0
            for (c0, nn) in segs:
                dma("pool", lambda e, o=o, c0=c0, nn=nn: e.dma_start(out=w[:, :, o:o + nn], in_=w_in_v[:, :, c0:c0 + nn]), writes=w.b)
                o += nn
            for stile in range(16):
                bi = next_bank()
                for c in range(16):
                    op("pe", lambda e, c=c, bi=bi, stile=stile: e.matmul(bank(bi), xaT[:, c, stile * 128:(stile + 1) * 128], w[:, c, :], start=(c == 0), stop=(c == 15)),
                       reads=w.b + xaT.b, writes=[bankB[bi]])
                s_ = stg[nst[0] % 4]; nst[0] += 1
                evac(s_[:], bank(bi), [bankB[bi]], s_.b)
                dma("sp", lambda e, s_=s_, stile=stile, dst=dst, dcol=dcol: e.dma_start(out=dst[stile * 128:(stile + 1) * 128, dcol:dcol + 512], in_=s_[:]),
                    reads=s_.b, writes=[qkvB[key]])
        sch.barrier()
    ph1.close()

    state = dict(nc=nc, es=es, sch=sch, op=op, dma=dma, sb=sb, bank=bank, bankB=bankB, evac=evac, PA=PA, PB=PB, PAb=PAb, PBb=PBb,
                 identf=identf, identb=identb, trib=trib, ntrib=ntrib, onesb=onesb, CB=CB, cm_f=cm_f, glT=glT)

    ph3 = ExitStack()
    oT = sb("oT", [128, 16, NOWN], BF16, 16, ph3)

    def sb_attention():
        with ExitStack() as st:
            masks = sb("sbm", [128, 24, 512], BF16, 1, st)
            dma("sp", lambda e: e.dma_start(out=masks[:], in_=sbmask_d.rearrange("u p t -> p u t")), writes=masks.b)
            k_ = sb("sbk", [128, 4, S], BF16, 1, st)
            q_ = sb("sbq", [128, 4, NOWN], BF16, 1, st)
            v_ = sb("sbv", [128, 16, 512], BF16, 1, st)
            NB = 6
            ez = [sb(f"ez{i}", [128, 512], F32, 1, st) for i in range(NB)]
            lpb = [sb(f"lpb{i}", [128, 512], BF16, 1, st) for i in range(NB)]
            ew = [sb(f"ew{i}", [128, 512], F32, 1, st) for i in range(NB)]
            wt = [sb(f"wt{i}", [128, 512], BF16, 1, st) for i in range(NB)]
            u = 0
            for stage in range(2):
                h0 = stage * 4
                dma("sp", lambda e: e.dma_start(out=k_[:], in_=KSd[h0 * 128:(h0 + 4) * 128, :].rearrange("(h d) s -> d h s", d=128)), reads=[qkvB["KS"]], writes=k_.b)
                dma("sp", lambda e: e.dma_start(out=q_[:], in_=QSd[h0 * 128:(h0 + 4) * 128, :].rearrange("(h d) s -> d h s", d=128)), reads=[qkvB["QS"]], writes=q_.b)
                dma("sp", lambda e: e.dma_start(out=v_[:], in_=VSd[:, h0 * 128:(h0 + 4) * 128].rearrange("(t p) c -> p t c", p=128)), reads=[qkvB["VS"]], writes=v_.b)
                chains = [(j, 1) for j in range(4)] + [(j, 0) for j in range(4)]
                for g0 in range(0, 8, 3):
                    grp = chains[g0:g0 + 3]
                    lens = [8 if ci == 0 else 16 for (_, ci) in grp]
                    def fa(act_):
                        for c in act_:
                            op("pe", lambda e: e.matmul(bank(c["bZ"]), k_[:, c["j"], c["kb"] * 128:(c["kb"] + 1) * 128], q_[:, c["j"], c["ci"] * 512:(c["ci"] + 1) * 512], start=True, stop=True),
                               reads=k_.b + q_.b, writes=[bankB[c["bZ"]]])
                            op("act", lambda e: e.activation(out=ez[c["x"]][:], in_=bank(c["bZ"]), func=AF.Exp, scale=SCALE), reads=[bankB[c["bZ"]]], writes=ez[c["x"]].b)

                    def fb(act_):
                        for c in act_:
                            x = c["x"]
                            if c["general"]:
                                op("dve", lambda e: e.tensor_tensor(out=ez[x][:], in0=ez[x][:], in1=masks[:, c["mi"], :], op=ALU.mult), reads=ez[x].b + masks.b, writes=ez[x].b)
                            op("act", lambda e: e.activation(out=lpb[x][:], in_=ez[x][:], func=AF.Ln, bias=1.0), reads=ez[x].b, writes=lpb[x].b)

                    def b1(act_):
                        for c in act_:
                            x = c["x"]; bA = c["bA"]
                            op("pe", lambda e: e.matmul(bank(bA), trib, lpb[x][:], start=c["first"], stop=True, skip_group_check=True), reads=lpb[x].b + CB, writes=[bankB[bA]])
                            op("act", lambda e: e.activation(out=ew[x][:], in_=bank(bA), func=AF.Exp, scale=-1.0), reads=[bankB[bA]], writes=ew[x].b)

                    def b2(act_):
                        for c in act_:
                            x = c["x"]; bA = c["bA"]
                            if c["kb"] > 0:
                                op("pe", lambda e: e.matmul(bank(bA), ntrib, lpb[x][:], start=False, stop=True, skip_group_check=True), reads=lpb[x].b + CB, writes=[bankB[bA]])

                    def b3(act_):
                        for c in act_:
                            x = c["x"]; bO = c["bO"]
                            op("dve", lambda e: e.tensor_tensor(out=wt[x][:], in0=ez[x][:], in1=ew[x][:], op=ALU.mult), reads=ez[x].b + ew[x].b, writes=wt[x].b)
                            op("pe", lambda e: e.matmul(bank(bO), v_[:, c["kb"], c["j"] * 128:(c["j"] + 1) * 128], wt[x][:], start=c["first"], stop=(c["kb"] == 0)),
                               reads=v_.b + wt[x].b, writes=[bankB[bO]])
                            if c["kb"] == 0:
                                hh = 8 + h0 + c["j"]
                                evac(oT[:, hh, c["ci"] * 512:(c["ci"] + 1) * 512], bank(bO), [bankB[bO]], [oT.b[hh]])
                        conv_step(5, dep=wt[act_[-1]["x"]].b)

                    prev = None
                    for s_ in range(max(lens)):
                        act_ = []
                        for i, (j, ci) in enumerate(grp):
                            if s_ >= lens[i]:
                                continue
                            kb = lens[i] - 1 - s_
                            mi = kb if ci == 0 else 8 + kb
                            general = (ci == 0) or (kb >= 8)
                            act_.append(dict(i=i, j=j, ci=ci, kb=kb, mi=mi, general=general, first=(s_ == 0), bA=2 + i, bO=5 + i, x=u % NB, bZ=u % 2))
                            u += 1
                        if prev is not None:
                            b1(prev)
                        fa(act_)
                        if prev is not None:
                            b2(prev)
                            b3(prev)
                        fb(act_)
                        prev = act_
                    b1(prev); b2(prev); b3(prev)
            sch.barrier()

    def bc(ap, pattern, off=0):
        return bass.AP(ap.tensor, ap.offset + off, [list(ap.ap[0])] + [list(p) for p in pattern])


    def convert_tables():
        for i in range(16):
            src, dst = (pu, pu16) if i < 8 else (pv, pv16)
            r0 = (i % 8) * 2048
            dma("pool", lambda e: e.dma_start(out=dst[r0:r0 + 2048, :], in_=src[r0:r0 + 2048, :]), writes=[tabB[i]])

    def nsa_attention():
        with ExitStack() as st:
            esel = sb("esel", [32, 2048], BF16, 1, st)
            ovl = sb("ovl", [128, 32], F32, 1, st)
            smul = sb("smul", [128, 8, 32], F32, 1, st)
            sadd = sb("sadd", [128, 8, 32], F32, 1, st)
            dma("pool", lambda e: e.dma_start(out=esel[:], in_=esel_d[:, :]), writes=esel.b)
            dma("sp", lambda e: e.dma_start(out=ovl[:], in_=ovl_d[:, :]), writes=ovl.b)
            dma("sp", lambda e: e.dma_start(out=smul[:], in_=selmul_d.rearrange("(t p) j -> p t j", p=128)), writes=smul.b)
            dma("sp", lambda e: e.dma_start(out=sadd[:], in_=seladd_d.rearrange("(t p) j -> p t j", p=128)), writes=sadd.b)
            w1s = sb("w1s", [128, 32, 128], BF16, 1, st)
            w2s = sb("w2s", [128, 128], BF16, 1, st)
            posTs = sb("posTs", [128, 32], BF16, 1, st)
            kT = sb("nk", [128, 4, S], BF16, 4, st)
            qT = sb("nq", [128, 4, NOWN], BF16, 1, st)
            v2 = sb("nv", [128, 16, 256], BF16, 1, st)
            kccT = sb("kccT", [128, 128], BF16, 1, st)
            vcc = sb("vcc", [128, 128], BF16, 1, st)
            b1s = sb("b1s", [128, 1], F32, 1, st)
            hT = sb("hT", [128, 128], BF16, 1, st)
            oaccC = sb("oaccC", [128, 4, 512], BF16, 4, st)
            negselT = sb("negselT", [32, 512], BF16, 1, st)
            impT = sb("impT", [32, 512], F32, 1, st)
            e32 = [sb(f"e32_{i}", [128, 512], F32, 1, st) for i in range(2)]
            ebf = [sb(f"ebf{i}", [128, 512], BF16, 1, st) for i in range(5)]
            biasb = [sb(f"biasb{i}", [128, 512], BF16, 1, st) for i in range(10)]
            dens = sb("dens", [128, 512], F32, 1, st)
            eaccs = [sb(f"eacc{i}", [128, 512], F32, 1, st) for i in range(2)]
            lsb = [sb(f"lsb{i}", [128, 512], F32, 1, st) for i in range(4)]
            l2 = sb("l2", [128, 512], F32, 1, st)
            rden = sb("rden", [128, 512], F32, 1, st)
            tmpg = sb("tmpg", [128, 512], F32, 1, st)
            Sg = sb("Sg", [128, 512], F32, 1, st)
            p32 = sb("p32", [128, 512], F32, 1, st)
            acc = sb("acc", [128, 512], F32, 1, st)
            tmp2 = sb("tmp2", [128, 512], F32, 1, st)
            sc = sb("sc", [128, 32], F32, 1, st)
            wks = sb("wks", [128, 32], F32, 1, st)
            v8a = sb("v8a", [128, 8], F32, 1, st)
            v8b = sb("v8b", [128, 8], F32, 1, st)
            nsl = sb("nsl", [128, 32], F32, 1, st)
            cnt = dict(L=0, d=0, o=0, e=0, b=0)

            def finalize(bd, bo, gcol, ci, out_ap, out_bufs, want_rden=False):
                op("dve", lambda e: e.tensor_scalar_max(out=dens[:], in0=bank(bd), scalar1=1e-18), reads=[bankB[bd]], writes=dens.b)
                op("act", lambda e: e.activation(out=l2[:], in_=dens[:], func=AF.Ln), reads=dens.b, writes=l2.b)
                if want_rden:
                    op("act", lambda e: e.activation(out=rden[:], in_=l2[:], func=AF.Exp, scale=-1.0), reads=l2.b, writes=rden.b)
                op("pe", lambda e: e.matmul(bank(7), identf[0:24, gcol:gcol + 1].to_broadcast([24, 128]), glT[0:24, ci * 512:(ci + 1) * 512], start=True, stop=True),
                   reads=CB + glT.b, writes=[bankB[7]])
                op("dve", lambda e: e.tensor_tensor(out=tmpg[:], in0=l2[:], in1=bank(7), op=ALU.add), reads=l2.b + [bankB[7]], writes=tmpg.b)
                op("act", lambda e: e.activation(out=Sg[:], in_=tmpg[:], func=AF.Exp, scale=-1.0), reads=tmpg.b, writes=Sg.b)
                op("dve", lambda e: e.tensor_tensor(out=out_ap, in0=bank(bo), in1=Sg[:], op=ALU.mult), reads=[bankB[bo]] + Sg.b, writes=out_bufs)

            fq = []

            def fin_gen(bd, bo, gcol, ci, out_ap, out_bufs, post):
                yield op("dve", lambda e: e.tensor_scalar_max(out=dens[:], in0=bank(bd), scalar1=1e-18), reads=[bankB[bd]], writes=dens.b)
                yield op("act", lambda e: e.activation(out=l2[:], in_=dens[:], func=AF.Ln), reads=dens.b, writes=l2.b)
                yield op("pe", lambda e: e.matmul(bank(7), identf[0:24, gcol:gcol + 1].to_broadcast([24, 128]), glT[0:24, ci * 512:(ci + 1) * 512], start=True, stop=True),
                         reads=CB + glT.b, writes=[bankB[7]])
                yield op("dve", lambda e: e.tensor_tensor(out=tmpg[:], in0=l2[:], in1=bank(7), op=ALU.add), reads=l2.b + [bankB[7]], writes=tmpg.b)
                yield op("act", lambda e: e.activation(out=Sg[:], in_=tmpg[:], func=AF.Exp, scale=-1.0), reads=tmpg.b, writes=Sg.b)
                yield op("dve", lambda e: e.tensor_tensor(out=out_ap, in0=bank(bo), in1=Sg[:], op=ALU.mult), reads=[bankB[bo]] + Sg.b, writes=out_bufs)
                for p_ in post:
                    yield p_()

            def pump(n=1):
                for _ in range(n):
                    advanced = False
                    while fq and not advanced:
                        try:
                            next(fq[0])
                            advanced = True
                        except StopIteration:
                            fq.pop(0)
                    if not advanced:
                        return

            def drain(keep=0):
                while len(fq) > keep:
                    try:
                        next(fq[0])
                    except StopIteration:
                        fq.pop(0)

            for g in range(2):
                for ty in range(4):
                    r0 = (ty * 2 + g) * 128
                    dma("sp", lambda e: e.dma_start(out=kT[:, ty, :], in_=KNd[r0:r0 + 128, :]), reads=[qkvB["KN"]], writes=[kT.b[ty]])
                dma("sp", lambda e: e.dma_start(out=qT[:], in_=QNd[g * 512:(g + 1) * 512, :].rearrange("(h d) s -> d h s", d=128)), reads=[qkvB["QN"]], writes=qT.b)
                dma("sp", lambda e: e.dma_start(out=v2[:, :, 0:128], in_=VNd[:, 128 * g:128 * g + 128].rearrange("(t p) c -> p t c", p=128)), reads=[qkvB["VN"]], writes=v2.b)
                dma("sp", lambda e: e.dma_start(out=v2[:, :, 128:256], in_=VNd[:, 256 + 128 * g:256 + 128 * g + 128].rearrange("(t p) c -> p t c", p=128)), reads=[qkvB["VN"]], writes=v2.b)
                for kv in range(2):
                    dma("pool", lambda e: e.dma_start(out=w1s[:], in_=w1[kv].rearrange("(l d) f -> d l f", d=128)), writes=w1s.b)
                    dma("pool", lambda e: e.dma_start(out=w2s[:], in_=w2[kv]), writes=w2s.b)
                    dma("pool", lambda e: e.dma_start(out=posTs[:], in_=posT[kv]), writes=posTs.b)
                    for l in range(32):
                        op("pe", lambda e: e.matmul(bank(6)[:, 0:1], w1s[:, l, :], posTs[:, l:l + 1], start=(l == 0), stop=(l == 31)),
                           reads=w1s.b + posTs.b, writes=[bankB[6]])
                    op("dve", lambda e: e.tensor_copy(out=b1s[:, 0:1], in_=bank(6)[:, 0:1]), reads=[bankB[6]], writes=b1s.b)
                    for l in range(32):
                        op("pe", lambda e: e.matmul(bank(7)[:, 0:127], w1s[:, l, :], kT[:, kv, l:l + 16 * 126 + 1:16], start=(l == 0), stop=(l == 31)),
                           reads=w1s.b + [kT.b[kv]], writes=[bankB[7]])
                    op("act", lambda e: e.activation(out=hT[:, 0:127], in_=bank(7)[:, 0:127], func=AF.Gelu_apprx_tanh, bias=b1s[:, 0:1]),
                       reads=[bankB[7]] + b1s.b, writes=hT.b)
                    if kv == 0:
                        op("pe", lambda e: e.matmul(bank(6)[:, 0:127], w2s[:], hT[:, 0:127], start=True, stop=True), reads=w2s.b + hT.b, writes=[bankB[6]])
                        op("dve", lambda e: e.tensor_copy(out=kccT[:, 0:127], in_=bank(6)[:, 0:127]), reads=[bankB[6]], writes=kccT.b)
                    else:
                        op("pe", lambda e: e.matmul(bank(6)[0:127, 0:128], hT[:, 0:127], w2s[:], start=True, stop=True), reads=w2s.b + hT.b, writes=[bankB[6]])
                        op("dve", lambda e: e.tensor_copy(out=vcc[0:127, :], in_=bank(6)[0:127, 0:128]), reads=[bankB[6]], writes=vcc.b)
                for ci in range(2):
                    qs = slice(ci * 512, (ci + 1) * 512)
                    drain(0)
                    for r in range(4):
                        h = 4 * g + r
                        bL = cnt["L"] % 2; cnt["L"] += 1
                        bd = 2 + cnt["d"] % 2; cnt["d"] += 1
                        bo = 4 + cnt["o"] % 2; cnt["o"] += 1
                        bt = biasb[cnt["b"] % 10]; cnt["b"] += 1
                        ee = e32[cnt["e"] % 2]; eb = ebf[cnt["e"] % 5]; cnt["e"] += 1
                        dma("sp", lambda e: e.dma_start(out=bt[0:127, :], in_=bass.AP(Bc[ci][h], 2016, [[PTC, 127], [1, 512]])), reads=scr_parts, writes=bt.b)
                        op("pe", lambda e: e.matmul(bank(bL)[0:127, :], kccT[:, 0:127], qT[:, r, qs], start=True, stop=False), reads=kccT.b + qT.b, writes=[bankB[bL]])
                        op("pe", lambda e: e.matmul(bank(bL)[0:127, :], identb[0:127, 0:127], bt[0:127, :], start=False, stop=True), reads=CB + bt.b, writes=[bankB[bL]])
                        op("act", lambda e: e.activation(out=ee[0:127, :], in_=bank(bL)[0:127, :], func=AF.Exp, scale=SCALE), reads=[bankB[bL]], writes=ee.b)
                        op("dve", lambda e: e.tensor_copy(out=eb[0:127, :], in_=ee[0:127, :]), reads=ee.b, writes=eb.b)
                        op("pe", lambda e: e.matmul(bank(bd), onesb[0:127, :], eb[0:127, :], start=True, stop=True), reads=CB + eb.b, writes=[bankB[bd]])
                        op("pe", lambda e: e.matmul(bank(bo), vcc[0:127, :], eb[0:127, :], start=True, stop=True), reads=vcc.b + eb.b, writes=[bankB[bo]])
                        finalize(bd, bo, 3 * h + 0, ci, oaccC[:, r, :], [oaccC.b[r]], want_rden=True)
                        op("dve", lambda e: e.tensor_tensor(out=p32[0:127, :], in0=ee[0:127, :], in1=rden[0:127, :], op=ALU.mult), reads=ee.b + rden.b, writes=p32.b)
                        op("pe", lambda e: e.matmul(bank(6)[0:32, :], ovl[0:127, :], p32[0:127, :], start=(r == 0), stop=(r == 3)), reads=ovl.b + p32.b, writes=[bankB[6]])
                    op("act", lambda e: e.activation(out=impT[:], in_=bank(6)[0:32, :], func=AF.Copy), reads=[bankB[6]], writes=impT.b)
                    for tt in range(4):
                        op("pe", lambda e: e.transpose(out=bank(7)[:, 0:32], in_=impT[0:32, tt * 128:(tt + 1) * 128], identity=identf[0:32, 0:32]),
                           reads=impT.b + CB, writes=[bankB[7]])
                        op("dve", lambda e: e.tensor_tensor(out=sc[:], in0=bank(7)[:, 0:32], in1=smul[:, ci * 4 + tt, :], op=ALU.mult), reads=[bankB[7]] + smul.b, writes=sc.b)
                        op("dve", lambda e: e.tensor_tensor(out=sc[:], in0=sc[:], in1=sadd[:, ci * 4 + tt, :], op=ALU.add), reads=sc.b + sadd.b, writes=sc.b)
                        op("dve", lambda e: e.max(out=v8a[:], in_=sc[:]), reads=sc.b, writes=v8a.b)
                        op("dve", lambda e: e.match_replace(out=wks[:], in_to_replace=v8a[:], in_values=sc[:], imm_value=-3e38), reads=sc.b + v8a.b, writes=wks.b)
                        op("dve", lambda e: e.max(out=v8b[:], in_=wks[:]), reads=wks.b, writes=v8b.b)
                        op("dve", lambda e: e.tensor_scalar(out=nsl[:], in0=sc[:], scalar1=v8b[:, 7:8], scalar2=-1.0, op0=ALU.is_ge, op1=ALU.add),
                           reads=sc.b + v8b.b, writes=nsl.b)
                        op("pe", lambda e: e.transpose(out=bank(7)[0:32, 128:256], in_=nsl[:, 0:32], identity=identf), reads=nsl.b + CB, writes=[bankB[7]])
                        op("act", lambda e: e.activation(out=negselT[0:32, tt * 128:(tt + 1) * 128], in_=bank(7)[0:32, 128:256], func=AF.Copy, scale=-NEG),
                           reads=[bankB[7]], writes=negselT.b)
                    for r in range(4):
                        h = 4 * g + r
                        for br in range(2):
                            if br == 0:
                                kbs = list(range(0, 8)) if ci == 0 else list(range(4, 16))
                            else:
                                kbs = list(range(0, 8)) if ci == 0 else list(range(0, 16))
                            bd = 2 + cnt["d"] % 2; cnt["d"] += 1
                            bo = 4 + cnt["o"] % 2; cnt["o"] += 1
                            Bh = (Bw if br == 0 else Bs)[ci][h]
                            eacc = eaccs[cnt["d"] % 2]
                            kty = 3 if br == 0 else 2
                            vo = 128 if br == 0 else 0
                            drain(1)
                            pending = []
                            for n, kb in enumerate(kbs):
                                bL = cnt["L"] % 2; cnt["L"] += 1
                                bt = biasb[cnt["b"] % 10]; cnt["b"] += 1
                                eb = ebf[cnt["e"] % 5]; cnt["e"] += 1
                                dma("sp", lambda e: e.dma_start(out=bt[:], in_=bass.AP(Bh, 2047 - 128 * kb, [[PT, 128], [1, 512]])), reads=scr_parts, writes=bt.b)
                                op("pe", lambda e: e.matmul(bank(bL), kT[:, kty, kb * 128:(kb + 1) * 128], qT[:, r, qs], start=True, stop=(br == 0)),
                                   reads=[kT.b[kty]] + qT.b, writes=[bankB[bL]])
                                if br == 1:
                                    op("pe", lambda e: e.matmul(bank(bL), esel[0:32, kb * 128:(kb + 1) * 128], negselT[0:32, :], start=False, stop=True),
                                       reads=esel.b + negselT.b, writes=[bankB[bL]])
                                ls_ = lsb[cnt["L"] % 4]
                                op("dve", lambda e: e.tensor_tensor(out=ls_[:], in0=bank(bL), in1=bt[:], op=ALU.add), reads=[bankB[bL]] + bt.b, writes=ls_.b)
                                if len(pending) > 1:
                                    pending.pop(0)()
                                op("act", lambda e: e.activation(out=eb[:], in_=ls_[:], func=AF.Exp, scale=SCALE), reads=ls_.b, writes=eb.b)
                                pump(1)
                                for _d in range(1):
                                    op("pe", lambda e: e.matmul(bank(6), identb, cm_b[:, 0:512], start=True, stop=True), reads=CB, writes=[bankB[6]])
                                if n % 4 != 3:
                                    conv_step(dep=eb.b)

                                def pend(eb=eb, n=n, kb=kb):
                                    op("pe", lambda e: e.matmul(bank(bd), onesb, eb[:], start=(n == 0), stop=(n == len(kbs) - 1)), reads=CB + eb.b, writes=[bankB[bd]])
                                    op("pe", lambda e: e.matmul(bank(bo), v2[:, kb, vo:vo + 128], eb[:], start=(n == 0), stop=(n == len(kbs) - 1)),
                                       reads=v2.b + eb.b, writes=[bankB[bo]])
                                pending.append(pend)
                            while pending:
                                pending.pop(0)()
                            if br == 0:
                                fq.append(fin_gen(bd, bo, 3 * h + 2, ci, acc[:], acc.b, []))
                            else:
                                def post1():
                                    return op("dve", lambda e: e.tensor_tensor(out=acc[:], in0=acc[:], in1=tmp2[:], op=ALU.add), reads=acc.b + tmp2.b, writes=acc.b)

                                def post2(h=h, qs=qs, r=r):
                                    return op("dve", lambda e: e.tensor_tensor(out=oT[:, h, qs], in0=acc[:], in1=oaccC[:, r, :], op=ALU.add), reads=acc.b + [oaccC.b[r]], writes=[oT.b[h]])
                                fq.append(fin_gen(bd, bo, 3 * h + 1, ci, tmp2[:], tmp2.b, [post1, post2]))
            drain(0)
            sch.barrier()

    sb_attention()
    nsa_attention()

    outs = []
    if stop in ("sb", "nsa"):
        with ExitStack() as st:
            of = sb("dbg_of", [128, 8, NOWN], F32, 1, st)
            lo = 8 if stop == "sb" else 0
            op("dve", lambda e: e.tensor_copy(out=of[:], in_=oT[:, lo:lo + 8, :]), reads=oT.b, writes=of.b)
            outs.append(dma("sp", lambda e: e.dma_start(out=dbg_t.rearrange("(h d) t -> d h t", d=128), in_=of[:]), reads=of.b))
            sch.finish(outs)
        ph3.close()
        es.close()
        return nc

    def rms_ap(x_ap, x_bufs, gt, out_ap, out_bufs, junk, ssq, rs):
        op("act", lambda e: e.activation(out=junk[:], in_=x_ap, func=AF.Square, accum_out=ssq[:, 0:1]), reads=x_bufs, writes=junk.b + ssq.b)
        op("dve", lambda e: e.tensor_scalar(out=rs[:, 0:1], in0=ssq[:, 0:1], scalar1=1.0 / D, scalar2=EPS, op0=ALU.mult, op1=ALU.add), reads=ssq.b, writes=rs.b)
        op("act", lambda e: e.activation(out=rs[:, 0:1], in_=rs[:, 0:1], func=AF.Sqrt), reads=rs.b, writes=rs.b)
        op("dve", lambda e: e.reciprocal(out=rs[:, 0:1], in_=rs[:, 0:1]), reads=rs.b, writes=rs.b)
        op("dve", lambda e: e.scalar_tensor_tensor(out=out_ap, in0=x_ap, scalar=rs[:, 0:1], in1=gt[:], op0=ALU.mult, op1=ALU.mult),
           reads=x_bufs + rs.b + gt.b, writes=out_bufs)

    with ExitStack() as st:
        wo = sb("wo", [128, 16, D], BF16, 4, st)
        w_out_v = w_out.rearrange("(c p) n -> p c n", p=128)
        for nb in range(4):
            dma("pool", lambda e: e.dma_start(out=wo[:, :, nb * 512:(nb + 1) * 512], in_=w_out_v[:, :, nb * 512:(nb + 1) * 512]), writes=[wo.b[nb]])
        for tt in range(8):
            dma("sp", lambda e: e.dma_start(out=hres[:, tt, :], in_=xo[tt * 128:(tt + 1) * 128, :]), writes=[hres.b[tt]])
        nbk4 = 0
        for nb in range(4):
            for tt in range(8):
                bi = nbk4 % 8; nbk4 += 1
                for f in range(16):
                    op("pe", lambda e: e.matmul(bank(bi), oT[:, f, tt * 128:(tt + 1) * 128], wo[:, f, nb * 512:(nb + 1) * 512], start=(f == 0), stop=(f == 15)),
                       reads=[oT.b[f], wo.b[nb]], writes=[bankB[bi]])
                op("dve", lambda e: e.tensor_tensor(out=hres[:, tt, nb * 512:(nb + 1) * 512], in0=hres[:, tt, nb * 512:(nb + 1) * 512], in1=bank(bi), op=ALU.add),
                   reads=[hres.b[tt], bankB[bi]], writes=[hres.b[tt]])
                conv_step(1, dep=[hres.b[tt]])
        sch.barrier()
    ph3.close()

    if stop == "h1":
        outs = [dma("sp", lambda e: e.dma_start(out=dbg_t.rearrange("(t p) n -> p t n", p=128), in_=hres[:]), reads=hres.b)]
        sch.finish(outs)
        es.close()
        return nc

    with ExitStack() as st:
        gkv = load_gain("gkv", g_mkv, st)
        junk = sb("p5junk", [128, D], BF16, 1, st)
        ssq = sb("p5ssq", [128, 1], F32, 1, st)
        rs = sb("p5rs", [128, 1], F32, 1, st)
        xt = sb("p5xt", [128, D], F32, 1, st)
        xn = sb("p5xn", [128, D], BF16, 1, st)
        memT = sb("memT", [128, 16, 256], BF16, 1, st)
        wk_ = sb("p5wk", [128, 16, 512], BF16, 1, st)
        wv_ = sb("p5wv", [128, 16, 512], BF16, 1, st)
        wq_ = sb("p5wq", [128, 16, 512], BF16, 1, st)
        wo_ = sb("p5wo", [128, 4, D], BF16, 1, st)
        dma("pool", lambda e: e.dma_start(out=wk_[:], in_=wmk.rearrange("(c p) n -> p c n", p=128)), writes=wk_.b)
        dma("pool", lambda e: e.dma_start(out=wv_[:], in_=wmv.rearrange("(c p) n -> p c n", p=128)), writes=wv_.b)
        dma("pool", lambda e: e.dma_start(out=wq_[:], in_=wmq.rearrange("(c p) n -> p c n", p=128)), writes=wq_.b)
        dma("pool", lambda e: e.dma_start(out=wo_[:], in_=wmo.rearrange("(c p) n -> p c n", p=128)), writes=wo_.b)
        kmT = sb("kmT", [128, 4, 256], BF16, 1, st)
        vm = sb("vm", [128, 2, 512], BF16, 1, st)
        hnT = sb("p5hnT", [128, 16, 512], BF16, 1, st)
        qmT = sb("qmT", [128, 4, 512], BF16, 1, st)
        omT = sb("omT", [128, 4, 512], BF16, 1, st)
        em = [sb(f"em{i}", [128, 512], BF16, 1, st) for i in range(2)]
        l2m = sb("l2m", [128, 512], F32, 1, st)
        rdm = sb("rdm", [128, 512], F32, 1, st)
        for mt in range(2):
            dma("sp", lambda e: e.dma_start(out=xt[:], in_=memb[mt * 128:(mt + 1) * 128, :]), writes=xt.b)
            rms_ap(xt[:], xt.b, gkv, xn[:], xn.b, junk, ssq, rs)
            transpose_tile(xn, memT, mt * 128, PAb, bankB[0:2])
        gq = gkv
        dma("sp", lambda e: e.dma_start(out=gq[:], in_=bass.AP(g_mq.tensor, 0, [[0, 128], [1, D]])), writes=gq.b)
        for hh in range(4):
            for c in range(16):
                op("pe", lambda e: e.matmul(bank(4)[:, 0:256], wk_[:, c, hh * 128:(hh + 1) * 128], memT[:, c, :], start=(c == 0), stop=(c == 15)),
                   reads=wk_.b + memT.b, writes=[bankB[4]])
            evac(kmT[:, hh, :], bank(4)[:, 0:256], [bankB[4]], kmT.b)
        for mt in range(2):
            for c in range(16):
                op("pe", lambda e: e.matmul(bank(5), memT[:, c, mt * 128:(mt + 1) * 128], wv_[:, c, :], start=(c == 0), stop=(c == 15)),
                   reads=wv_.b + memT.b, writes=[bankB[5]])
            evac(vm[:, mt, :], bank(5), [bankB[5]], vm.b)
        for tc in range(2):
            for t4 in range(4):
                tt = tc * 4 + t4
                rms_ap(hres[:, tt, :], [hres.b[tt]], gq, xn[:], xn.b, junk, ssq, rs)
                transpose_tile(xn, hnT, t4 * 128, PAb, bankB[0:2])
            for hh in range(4):
                for c in range(16):
                    op("pe", lambda e: e.matmul(bank(4), wq_[:, c, hh * 128:(hh + 1) * 128], hnT[:, c, :], start=(c == 0), stop=(c == 15)),
                       reads=wq_.b + hnT.b, writes=[bankB[4]])
                evac(qmT[:, hh, :], bank(4), [bankB[4]], qmT.b)
            for hh in range(4):
                for mt in range(2):
                    bL = 2 + mt
                    op("pe", lambda e: e.matmul(bank(bL), kmT[:, hh, mt * 128:(mt + 1) * 128], qmT[:, hh, :], start=True, stop=True), reads=kmT.b + qmT.b, writes=[bankB[bL]])
                    op("act", lambda e: e.activation(out=em[mt][:], in_=bank(bL), func=AF.Exp, scale=SCALE), reads=[bankB[bL]], writes=em[mt].b)
                    op("pe", lambda e: e.matmul(bank(6), onesb, em[mt][:], start=(mt == 0), stop=(mt == 1)), reads=CB + em[mt].b, writes=[bankB[6]])
                    op("pe", lambda e: e.matmul(bank(7), vm[:, mt, hh * 128:(hh + 1) * 128], em[mt][:], start=(mt == 0), stop=(mt == 1)), reads=vm.b + em[mt].b, writes=[bankB[7]])
                op("act", lambda e: e.activation(out=l2m[:], in_=bank(6), func=AF.Ln), reads=[bankB[6]], writes=l2m.b)
                op("act", lambda e: e.activation(out=rdm[:], in_=l2m[:], func=AF.Exp, scale=-1.0), reads=l2m.b, writes=rdm.b)
                op("dve", lambda e: e.tensor_tensor(out=omT[:, hh, :], in0=bank(7), in1=rdm[:], op=ALU.mult), reads=[bankB[7]] + rdm.b, writes=omT.b)
            for t4 in range(4):
                tt = tc * 4 + t4
                for nb in range(4):
                    bi = 4 + nb
                    for f in range(4):
                        op("pe", lambda e: e.matmul(bank(bi), omT[:, f, t4 * 128:(t4 + 1) * 128], wo_[:, f, nb * 512:(nb + 1) * 512], start=(f == 0), stop=(f == 3)),
                           reads=omT.b + wo_.b, writes=[bankB[bi]])
                    op("dve", lambda e: e.tensor_tensor(out=hres[:, tt, nb * 512:(nb + 1) * 512], in0=hres[:, tt, nb * 512:(nb + 1) * 512], in1=bank(bi), op=ALU.add),
                       reads=[hres.b[tt], bankB[bi]], writes=[hres.b[tt]])
        sch.barrier()

    if stop == "h2":
        outs = [dma("sp", lambda e: e.dma_start(out=dbg_t.rearrange("(t p) n -> p t n", p=128), in_=hres[:]), reads=hres.b)]
        sch.finish(outs)
        es.close()
        return nc

    outs = []
    with ExitStack() as st6:
        expT = sb("expT", [128, 8, 128], U32, 8, st6)
        gateT = sb("gateT", [128, 8, 128], F32, 8, st6)
        gffn = load_gain("gffn", g_ffn, st6)
        junk = sb("p6junk", [128, D], BF16, 1, st6)
        ssq = sb("p6ssq", [128, 1], F32, 1, st6)
        rs = sb("p6rs", [128, 1], F32, 1, st6)
        with ExitStack() as sa:
            qpT = sb("qpT", [128, 16, NOWN], BF16, 1, sa)
            with ExitStack() as sa1:
                hnT = sb("p6hnT", [128, 16, NOWN], BF16, 1, sa1)
                xn = sb("p6xn", [128, D], BF16, 1, sa1)
                wbs = [sb(f"p6wb{i}", [128, 16, 128], BF16, 1, sa1) for i in range(3)]
                for tt in range(8):
                    rms_ap(hres[:, tt, :], [hres.b[tt]], gffn, xn[:], xn.b, junk, ssq, rs)
                    if tt % 2 == 0:
                        transpose_tile(xn, hnT, tt * 128, PAb, bankB[0:2])
                    else:
                        transpose_tile(xn, hnT, tt * 128, PBb, bankB[4:6])
                wpq_v = wpq.rearrange("(c p) n -> p c n", p=128)
                nb_ = 0
                for hp in range(16):
                    w = wbs[hp % 3]
                    dma("pool", lambda e: e.dma_start(out=w[:], in_=wpq_v[:, :, hp * 128:(hp + 1) * 128]), writes=w.b)
                    for tc in range(2):
                        bi = nb_ % 8; nb_ += 1
                        for c in range(16):
                            op("pe", lambda e: e.matmul(bank(bi), w[:, c, :], hnT[:, c, tc * 512:(tc + 1) * 512], start=(c == 0), stop=(c == 15)),
                               reads=w.b + hnT.b, writes=[bankB[bi]])
                        evac(qpT[:, hp, tc * 512:(tc + 1) * 512], bank(bi), [bankB[bi]], qpT.b)
                sch.barrier()
            skTs = sb("skTs", [128, 16, 128], BF16, 1, sa)
            dma("pool", lambda e: e.dma_start(out=skTs[:], in_=skT.rearrange("k d n -> d k n")), writes=skTs.b)
            iota16 = sb("iota16", [128, 16], F32, 1, sa)
            dma("sp", lambda e: e.dma_start(out=iota16[:], in_=iota16_d[:, 0:16]), writes=iota16.b)
            thr16 = sb("thr16", [128, 16], F32, 1, sa)
            thrb = sb("thrb", [128, 16], F32, 1, sa)
            op("dve", lambda e: e.tensor_scalar(out=thr16[:], in0=iota16[:], scalar1=16.0, scalar2=None, op0=ALU.mult), reads=iota16.b, writes=thr16.b)
            op("dve", lambda e: e.tensor_scalar(out=thrb[:], in0=iota16[:], scalar1=-16.0, scalar2=None, op0=ALU.add), reads=iota16.b, writes=thrb.b)
            s_sb = sb("s_sb", [128, 2048], F32, 1, sa)
            cand = sb("cand", [128, 2048], F32, 1, sa)
            cmpw = sb("cmpw", [128, 2048], F32, 1, sa)
            prw = sb("prw", [128, 2048], F32, 1, sa)
            tops = sb("tops", [128, 256], F32, 16, sa)
            topi = sb("topi", [128, 256], U32, 16, sa)
            topf = sb("topf", [128, 256], F32, 1, sa)
            dtop = sb("dtop", [128, 256], F32, 1, sa)
            wk1s = [sb(f"wk1_{i}", [128, 128], F32, 1, sa) for i in range(4)]
            wk2s = [sb(f"wk2_{i}", [128, 256], F32, 1, sa) for i in range(4)]
            best = sb("best", [128, 128], F32, 8, sa)
            bidx = sb("bidx", [128, 128], U32, 8, sa)
            bf = sb("bf", [128, 128], F32, 1, sa)
            negm = sb("negm", [128, 8], F32, 1, sa)
            eg = sb("eg", [128, 128], F32, 1, sa)
            zz = sb("zz", [128, 8], F32, 1, sa)
            rz = sb("rz", [128, 8], F32, 1, sa)
            gate = sb("gate", [128, 128], F32, 1, sa)
            ap1 = sb("ap1", [128, 128], F32, 1, sa)
            bq = sb("bq", [128, 128], F32, 1, sa)
            ia = sb("ia", [128, 128], F32, 1, sa)
            ib = sb("ib", [128, 128], F32, 1, sa)
            expf = sb("expf", [128, 128], F32, 1, sa)
            P4 = [[256, 8], [16, 16], [1, 16]]
            rq = []

            def rpump(n=1):
                for _ in range(n):
                    adv = False
                    while rq and not adv:
                        try:
                            next(rq[0]); adv = True
                        except StopIteration:
                            rq.pop(0)
                    if not adv:
                        return

            def rdrain():
                while rq:
                    try:
                        next(rq[0])
                    except StopIteration:
                        rq.pop(0)

            for tt in range(8):
                for hp in range(16):
                    op("pe", lambda e: e.matmul(PA[:, hp * 128:(hp + 1) * 128], qpT[:, hp, tt * 128:(tt + 1) * 128], skTs[:, hp, :], start=True, stop=True),
                       reads=qpT.b + skTs.b, writes=[bankB[hp // 4]])
                op("act", lambda e: e.activation(out=s_sb[:, 0:1024], in_=PA[:, 0:1024], func=AF.Copy), reads=bankB[0:2], writes=s_sb.b)
                op("dve", lambda e: e.tensor_copy(out=s_sb[:, 1024:2048], in_=PA[:, 1024:2048]), reads=bankB[2:4], writes=s_sb.b)
                for hp0 in range(0, 16, 4):
                    G = []
                    for hp in range(hp0, hp0 + 4):
                        G.append(dict(seg=s_sb[:, hp * 128:(hp + 1) * 128], a=tops[:, hp * 16:hp * 16 + 8], b=tops[:, hp * 16 + 8:hp * 16 + 16],
                                      ia=topi[:, hp * 16:hp * 16 + 8], ib=topi[:, hp * 16 + 8:hp * 16 + 16], wk=wk1s[hp % 4], tb=[tops.b[hp]], ti=[topi.b[hp]]))
                    for g_ in G:
                        op("dve", lambda e: e.max(out=g_["a"], in_=g_["seg"]), reads=s_sb.b, writes=g_["tb"])
                    rpump(1)
                    for g_ in G:
                        op("dve", lambda e: e.match_replace(out=g_["wk"][:], in_to_replace=g_["a"], in_values=g_["seg"], imm_value=-3e38), reads=s_sb.b + g_["tb"], writes=g_["wk"].b)
                    rpump(1)
                    for g_ in G:
                        op("dve", lambda e: e.max_index(out=g_["ia"], in_max=g_["a"], in_values=g_["seg"]), reads=s_sb.b + g_["tb"], writes=g_["ti"])
                    rpump(1)
                    for g_ in G:
                        op("dve", lambda e: e.max(out=g_["b"], in_=g_["wk"][:]), reads=g_["wk"].b, writes=g_["tb"])
                    rpump(1)
                    for g_ in G:
                        op("dve", lambda e: e.max_index(out=g_["ib"], in_max=g_["b"], in_values=g_["seg"]), reads=s_sb.b + g_["tb"], writes=g_["ti"])
                    conv_step(10, dep=G[-1]["ti"])
                op("dve", lambda e: e.tensor_tensor(out=bc(cand[:], P4), in0=bc(tops[:], [[32, 8], [1, 16], [0, 16]]), in1=bc(tops[:], [[32, 8], [0, 16], [1, 16]], 16), op=ALU.add),
                   reads=tops.b, writes=cand.b)
                for h0_ in range(0, 8, 4):
                    G = []
                    for h in range(h0_, h0_ + 4):
                        G.append(dict(seg=cand[:, h * 256:(h + 1) * 256], a=best[:, h * 16:h * 16 + 8], b=best[:, h * 16 + 8:h * 16 + 16],
                                      ia=bidx[:, h * 16:h * 16 + 8], ib=bidx[:, h * 16 + 8:h * 16 + 16], wk=wk2s[h % 4], tb=[best.b[h]], ti=[bidx.b[h]]))
                    for g_ in G:
                        op("dve", lambda e: e.max(out=g_["a"], in_=g_["seg"]), reads=cand.b, writes=g_["tb"])
                    rpump(1)
                    for g_ in G:
                        op("dve", lambda e: e.match_replace(out=g_["wk"][:], in_to_replace=g_["a"], in_values=g_["seg"], imm_value=-3e38), reads=cand.b + g_["tb"], writes=g_["wk"].b)
                    rpump(1)
                    for g_ in G:
                        op("dve", lambda e: e.max_index(out=g_["ia"], in_max=g_["a"], in_values=g_["seg"]), reads=cand.b + g_["tb"], writes=g_["ti"])
                    rpump(1)
                    for g_ in G:
                        op("dve", lambda e: e.max(out=g_["b"], in_=g_["wk"][:]), reads=g_["wk"].b, writes=g_["tb"])
                    for g_ in G:
                        op("dve", lambda e: e.max_index(out=g_["ib"], in_max=g_["b"], in_values=g_["seg"]), reads=cand.b + g_["tb"], writes=g_["ti"])
                rdrain()
                op("dve", lambda e: e.tensor_scalar(out=negm[:], in0=bc(best[:], [[16, 8]]), scalar1=-1.0, scalar2=None, op0=ALU.mult), reads=best.b, writes=negm.b)
                for h in range(8):
                    op("act", lambda e: e.activation(out=eg[:, h * 16:(h + 1) * 16], in_=best[:, h * 16:(h + 1) * 16], func=AF.Exp, bias=negm[:, h:h + 1], accum_out=zz[:, h:h + 1]),
                       reads=best.b + negm.b, writes=eg.b + zz.b)
                op("dve", lambda e: e.tensor_copy(out=bf[:], in_=bidx[:]), reads=bidx.b, writes=bf.b)
                op("dve", lambda e: e.tensor_copy(out=topf[:], in_=topi[:]), reads=topi.b, writes=topf.b)
                def routeB(tt):
                    yield op("dve", lambda e: e.reciprocal(out=rz[:], in_=zz[:]), reads=zz.b, writes=rz.b)
                    yield op("dve", lambda e: e.tensor_tensor(out=bc(gate[:], [[16, 8], [1, 16]]), in0=bc(eg[:], [[16, 8], [1, 16]]), in1=bc(rz[:], [[1, 8], [0, 16]]), op=ALU.mult),
                       reads=eg.b + rz.b, writes=gate.b)
                    yield op("dve", lambda e: e.tensor_copy(out=dtop[:], in_=topf[:]), reads=topf.b, writes=dtop.b)
                    yield op("dve", lambda e: e.tensor_tensor(out=bc(dtop[:], [[16, 16], [1, 15]], 1), in0=bc(topf[:], [[16, 16], [1, 15]], 1), in1=bc(topf[:], [[16, 16], [1, 15]], 0), op=ALU.subtract),
                       reads=topf.b + dtop.b, writes=dtop.b)
                    yield op("dve", lambda e: e.tensor_tensor(out=bc(cmpw[:], P4), in0=bc(bf[:], [[16, 8], [1, 16], [0, 16]]), in1=bc(thr16[:], [[0, 8], [0, 16], [1, 16]]), op=ALU.is_ge),
                       reads=bf.b + thr16.b, writes=cmpw.b)
                    yield op("dve", lambda e: e.tensor_tensor(out=bc(prw[:], P4), in0=bc(cmpw[:], P4), in1=bc(dtop[:], [[32, 8], [0, 16], [1, 16]]), op=ALU.mult),
                       reads=cmpw.b + dtop.b, writes=prw.b)
                    yield op("dve", lambda e: e.tensor_reduce(out=ia[:], in_=bc(prw[:], [[16, 128], [1, 16]]), axis=AX.X, op=ALU.add), reads=prw.b, writes=ia.b)
                    yield op("dve", lambda e: e.tensor_reduce(out=ap1[:], in_=bc(cmpw[:], [[16, 128], [1, 16]]), axis=AX.X, op=ALU.add), reads=cmpw.b, writes=ap1.b)
                    yield op("dve", lambda e: e.scalar_tensor_tensor(out=bq[:], in0=ap1[:], scalar=-16.0, in1=bf[:], op0=ALU.mult, op1=ALU.add), reads=ap1.b + bf.b, writes=bq.b)
                    yield op("dve", lambda e: e.tensor_tensor(out=bc(cmpw[:], P4), in0=bc(bq[:], [[16, 8], [1, 16], [0, 16]]), in1=bc(thrb[:], [[0, 8], [0, 16], [1, 16]]), op=ALU.is_ge),
                       reads=bq.b + thrb.b, writes=cmpw.b)
                    yield op("dve", lambda e: e.tensor_tensor(out=bc(prw[:], P4), in0=bc(cmpw[:], P4), in1=bc(dtop[:], [[32, 8], [0, 16], [1, 16]], 16), op=ALU.mult),
                       reads=cmpw.b + dtop.b, writes=prw.b)
                    yield op("dve", lambda e: e.tensor_reduce(out=ib[:], in_=bc(prw[:], [[16, 128], [1, 16]]), axis=AX.X, op=ALU.add), reads=prw.b, writes=ib.b)
                    yield op("dve", lambda e: e.scalar_tensor_tensor(out=expf[:], in0=ia[:], scalar=128.0, in1=ib[:], op0=ALU.mult, op1=ALU.add), reads=ia.b + ib.b, writes=expf.b)
                    yield op("pe", lambda e: e.transpose(out=bank(7)[:, 0:128], in_=expf[:], identity=identf), reads=expf.b + CB, writes=[bankB[7]])
                    yield op("dve", lambda e: e.tensor_copy(out=expT[:, tt, :], in_=bank(7)[:, 0:128]), reads=[bankB[7]], writes=[expT.b[tt]])
                    yield op("pe", lambda e: e.transpose(out=bank(6)[:, 0:128], in_=gate[:], identity=identf), reads=gate.b + CB, writes=[bankB[6]])
                    yield op("act", lambda e: e.activation(out=gateT[:, tt, :], in_=bank(6)[:, 0:128], func=AF.Copy), reads=[bankB[6]], writes=[gateT.b[tt]])
                rq.append(routeB(tt))
            rdrain()
            sch.barrier()

        if stop == "route":
            with ExitStack() as sd:
                of = sb("dbg_of", [128, 8, 256], F32, 1, sd)
                op("dve", lambda e: e.tensor_copy(out=of[:, :, 0:128], in_=expT[:]), reads=expT.b, writes=of.b)
                op("dve", lambda e: e.tensor_copy(out=of[:, :, 128:256], in_=gateT[:]), reads=gateT.b, writes=of.b)
                outs.append(dma("sp", lambda e: e.dma_start(out=dbg_t.rearrange("p (a b) -> p a b", a=8), in_=of[:]), reads=of.b))
                sch.finish(outs)
            es.close()
            return nc

        with ExitStack() as sg:
            gfin = load_gain("gfin", g_fin, sg)
            c255 = sb("c255", [128, 255], F32, 1, sg)
            dma("sp", lambda e: e.dma_start(out=c255[:], in_=c255_d[:, :]), writes=c255.b)
            hds = [sb(f"hd{i}", [128, 1], F32, 1, sg) for i in range(4)]
            xn3 = sb("xn3", [128, D], BF16, 1, sg)
            NG = 8
            conv_step(NCH)
            Ug = [sb(f"Ug{i}", [128, D], BF16, 1, sg) for i in range(NG)]
            NGV = 12
            Vg = [sb(f"Vg{i}", [128, D], BF16, 1, sg) for i in range(NGV)]
            aT = [sb(f"aT{i}", [128, 1], F32, 1, sg) for i in range(8)]
            ga = [sb(f"ga{i}", [128, 1], F32, 1, sg) for i in range(8)]
            Lt = [sb(f"Lt{i}", [128, 128], BF16, 1, sg) for i in range(4)]
            h3 = sb("h3", [128, D], F32, 1, sg)
            yt = h3
            aB = [sb(f"aB{i}", [128, 1], F32, 1, sg) for i in range(8)]
            xn3b = sb("xn3b", [128, D], BF16, 1, sg)
            n = 0
            for tt in range(8):
                xn_ = xn3 if tt % 2 == 0 else xn3b
                rms_ap(hres[:, tt, :], [hres.b[tt]], gffn, xn_[:], xn_.b, junk, ssq, rs)
                stash = {}
                for t in range(128 + 2):
                    if t < 128:
                        U = Ug[n % NG]; V = Vg[n % NGV]; a_ = aT[n % 8]; b_ = aB[n % 8]; g_ = ga[n % 8]; L_ = Lt[n % 4]
                        n += 1
                        stash[t] = (V, g_, L_)
                        dma("pool", lambda e: e.indirect_dma_start(out=U[:, :], out_offset=None, in_=pu16[:, :],
                                                                    in_offset=bass.IndirectOffsetOnAxis(ap=expT[:, tt, t:t + 1], axis=0)),
                            reads=[expT.b[tt]] + tabB[0:512], writes=U.b)
                        dma("pool", lambda e: e.indirect_dma_start(out=V[:, :], out_offset=None, in_=pv16[:, :],
                                                                    in_offset=bass.IndirectOffsetOnAxis(ap=expT[:, tt, t:t + 1], axis=0)),
                            reads=[expT.b[tt]] + tabB[512:1024], writes=V.b)
                        for hf, acc_ in ((0, a_), (1, b_)):
                            for nb in range(2):
                                bi = 4 + hf * 2 + nb
                                c0 = hf * 1024 + nb * 512
                                op("pe", lambda e: e.matmul(bank(bi), identb[:, t:t + 1].to_broadcast([128, 128]), xn_[:, c0:c0 + 512], start=True, stop=True),
                                   reads=CB + xn_.b, writes=[bankB[bi]])
                            op("dve", lambda e: e.scalar_tensor_tensor(out=junk[:, hf * 1024:(hf + 1) * 1024], in0=U[:, hf * 1024:(hf + 1) * 1024], scalar=1.0,
                                                                       in1=PB[:, hf * 1024:(hf + 1) * 1024], op0=ALU.mult, op1=ALU.mult, accum_out=acc_[:, 0:1]),
                               reads=U.b + bankB[4 + hf * 2:6 + hf * 2], writes=junk.b + acc_.b)
                        op("act", lambda e: e.activation(out=g_[:, 0:1], in_=a_[:, 0:1], func=AF.Gelu_apprx_tanh, bias=b_[:, 0:1]), reads=a_.b + b_.b, writes=g_.b)
                    t1_ = t - 1
                    if 0 <= t1_ < 128:
                        V1, g1, L1 = stash[t1_]
                        hd = hds[t1_ % 4]
                        op("act", lambda e: e.activation(out=hd[:, 0:1], in_=g1[:, 0:1], func=AF.Copy, scale=gateT[:, tt, t1_:t1_ + 1]), reads=g1.b + [gateT.b[tt]], writes=hd.b)
                        op("act", lambda e: e.activation(out=L1[:], in_=c255[:, 127 - t1_:255 - t1_], func=AF.Copy, scale=hd[:, 0:1]), reads=c255.b + hd.b, writes=L1.b)
                    t2_ = t - 2
                    if 0 <= t2_ < 128:
                        V2, g2, L2 = stash[t2_]
                        for nb in range(4):
                            op("pe", lambda e: e.matmul(bank(nb), L2[:], V2[:, nb * 512:(nb + 1) * 512], start=(t2_ == 0), stop=(t2_ == 127)),
                               reads=L2.b + V2.b, writes=[bankB[nb]])
                op("dve", lambda e: e.tensor_tensor(out=h3[:], in0=hres[:, tt, :], in1=PA[:, 0:2048], op=ALU.add), reads=[hres.b[tt]] + bankB[0:4], writes=h3.b)
                if stop == "h3":
                    outs.append(dma("sp", lambda e: e.dma_start(out=dbg_t[tt * 128:(tt + 1) * 128, :], in_=h3[:]), reads=h3.b))
                rms_ap(h3[:], h3.b, gfin, yt[:], yt.b, junk, ssq, rs)
                outs.append(dma("sp", lambda e: e.dma_start(out=y[tt * 128:(tt + 1) * 128, :], in_=yt[:]), reads=yt.b))
            sch.finish(outs)
    es.close()
    return nc


def make_in_maps(inputs):
    f32 = lambda a: np.ascontiguousarray(a, dtype=np.float32)
    sc = shared_consts()
    shared = {
        "t5": f32(inputs["t5_table"]),
        "g_mix": f32(inputs["norm_mix"][0][None]), "g_mq": f32(inputs["norm_mem_q"][0][None]),
        "g_mkv": f32(inputs["norm_mem_kv"][0][None]), "g_ffn": f32(inputs["norm_ffn"][0][None]),
        "g_fin": f32(inputs["norm_final"][None]),
        "w_in": f32(inputs["w_in"][0]),
        "posT": f32(np.stack([inputs["cmp_pos_k"][0].T, inputs["cmp_pos_v"][0].T])),
        "w1": f32(np.stack([inputs["cmp_w1_k"][0], inputs["cmp_w1_v"][0]])),
        "w2": f32(np.stack([inputs["cmp_w2_k"][0], inputs["cmp_w2_v"][0]])),
        "w_out": f32(inputs["w_out"][0]),
        "wmq": f32(inputs["w_mem_q"][0]), "wmk": f32(inputs["w_mem_k"][0]), "wmv": f32(inputs["w_mem_v"][0]),
        "wmo": f32(inputs["w_mem_o"][0]),
        "wpq": f32(inputs["peer_w_q"][0]),
        "skT": f32(np.transpose(inputs["peer_sub_keys"][0].reshape(16, 128, 128), (0, 2, 1))),
        "pu": f32(inputs["peer_u"][0]), "pv": f32(inputs["peer_v"][0]),
        "cmat": sc["cmat"], "esel": sc["esel"], "overlap": sc["overlap"], "c255": sc["c255"], "iota16": sc["iota16"],
    }
    pc = [host_consts(0), host_consts(1)]
    maps = []
    x = inputs["x"]; mem = inputs["mem"]
    for c in range(8):
        b, par = c // 2, c % 2
        q1, q2 = par, 3 - par
        own = np.concatenate([np.arange(512 * q1, 512 * q1 + 512), np.arange(512 * q2, 512 * q2 + 512)])
        m = dict(shared)
        m["xa"] = f32(x[b])
        m["xo"] = f32(x[b][own])
        m["memb"] = f32(mem[b])
        m.update(pc[par])
        maps.append(m)
    return maps


def own_rows(par):
    q1, q2 = par, 3 - par
    return np.concatenate([np.arange(512 * q1, 512 * q1 + 512), np.arange(512 * q2, 512 * q2 + 512)])


_NC_CACHE = {}


def kernel(**inputs):
    if "nc" not in _NC_CACHE:
        _NC_CACHE["nc"] = build_program("all")
    nc = _NC_CACHE["nc"]
    maps = make_in_maps(inputs)
    res = run_bass_kernel_spmd(nc, maps, core_ids=list(range(8)))
    out = np.zeros((4, S, D), np.float32)
    for c in range(8):
        b, par = c // 2, c % 2
        out[b, own_rows(par)] = res.results[c]["y"]
    return out
```

```python
import math
from contextlib import ExitStack

import numpy as np
import ml_dtypes

import concourse.bass as bass
import concourse.mybir as mybir
from concourse.bass_utils import run_bass_kernel_spmd

F32 = mybir.dt.float32
BF16 = mybir.dt.bfloat16
U32 = mybir.dt.uint32
AF = mybir.ActivationFunctionType
ALU = mybir.AluOpType
AX = mybir.AxisListType

D = 2048
S = 2048
NOWN = 1024
DH = 128
SCALE = DH ** -0.5
EPS = 1e-6
NEG = -30000.0
IN_COLS = 5656
C_QN, C_KC, C_VC, C_KS, C_VS, C_KW, C_VW, C_G, C_QS, C_KSB, C_VSB = 0, 1024, 1280, 1536, 1792, 2048, 2304, 2560, 2584, 3608, 4632
LG = 2560
PT = 2720
PTC = LG + 16 * 128 + 64


class Buf:
    __slots__ = ("w", "r", "name")

    def __init__(self, name=""):
        self.w = None
        self.r = {}
        self.name = name


class Sched:
    def __init__(self, nc, es, n_dma=56):
        self.nc = nc
        self.E = dict(pe=nc.tensor, act=nc.scalar, dve=nc.vector, pool=nc.gpsimd, sp=nc.sync)
        self.sem = {k: es.enter_context(nc.semaphore(f"sem_{k}")) for k in self.E}
        self.cnt = {k: 0 for k in self.E}
        self.seen = {k: {} for k in self.E}
        self.slots = [[es.enter_context(nc.semaphore(f"dsem{i}")), 0, None] for i in range(n_dma)]
        self.rr = 0
        self.cslots = [[es.enter_context(nc.semaphore(f"csem{i}")), 0, None] for i in range(8)]
        self.crr = 0
        self.n_wait = 0

    def _wait(self, eng, deps):
        need = {}
        for ev in deps:
            if ev is None:
                continue
            key, sem, val = ev
            if key == eng and eng == "pe":
                continue
            if need.get(key, (None, 0))[1] < val:
                need[key] = (sem, val)
        seen = self.seen[eng]
        for key, (sem, val) in need.items():
            if seen.get(key, 0) >= val:
                continue
            self.E[eng].wait_ge(sem, val)
            self.n_wait += 1
            seen[key] = val

    @staticmethod
    def _deps(reads, writes):
        deps = []
        for b in reads:
            deps.append(b.w)
        for b in writes:
            deps.append(b.w)
            deps.extend(b.r.values())
        return deps

    @staticmethod
    def _mark(ev, reads, writes):
        for b in reads:
            b.r[ev[0]] = ev
        for b in writes:
            b.w = ev
            b.r = {}

    def op(self, eng, fn, reads=(), writes=()):
        self._wait(eng, self._deps(reads, writes))
        inst = fn(self.E[eng])
        self.cnt[eng] += 1
        ev = (eng, self.sem[eng], self.cnt[eng])
        inst.then_inc(self.sem[eng], 1)
        self._mark(ev, reads, writes)
        return ev

    def dma(self, q, fn, reads=(), writes=(), conv=False):
        if conv:
            slot = self.cslots[self.crr]
            key = f"c{self.crr}"
            self.crr = (self.crr + 1) % len(self.cslots)
        else:
            slot = self.slots[self.rr]
            key = f"d{self.rr}"
            self.rr = (self.rr + 1) % len(self.slots)
        deps = self._deps(reads, writes)
        deps.append(slot[2])
        self._wait(q, deps)
        inst = fn(self.E[q])
        slot[1] += 16
        ev = (key, slot[0], slot[1])
        inst.then_inc(slot[0], 16)
        slot[2] = ev
        self._mark(ev, reads, writes)
        return ev

    def barrier(self):
        evs = []
        for k in self.E:
            if self.cnt[k]:
                evs.append((k, self.sem[k], self.cnt[k]))
        for i, s in enumerate(self.slots):
            if s[2] is not None:
                evs.append(s[2])
        for k in self.E:
            self._wait(k, [e for e in evs if e[0] != k or k != "pe"])

    def finish(self, evs):
        self._wait("sp", evs)


class T:
    def __init__(self, t, nparts=1, name=""):
        self.t = t
        self.b = [Buf(f"{name}{i}") for i in range(nparts)]

    def __getitem__(self, k):
        return self.t[k]


def t5_bucket_np(dist):
    n = np.maximum(dist, 0)
    nf = np.maximum(n, 1).astype(np.float32)
    large = 16 + (np.log(nf / np.float32(16)) / np.float32(math.log(128 / 16)) * np.float32(16)).astype(np.int32)
    large = np.minimum(large, 31)
    return np.where(n < 16, n, large)


def host_consts(par):
    q = [par, 3 - par]
    qoff = [512 * q[0], 512 * q[1]]
    c = {}
    i = np.arange(128)[:, None]
    j = np.arange(512)[None, :]
    sbm = np.zeros((24, 128, 512), np.float32)
    u = 0
    for ci in range(2):
        for kb in range(8 if ci == 0 else 16):
            sbm[u] = ((128 * kb + i) < (qoff[ci] + j)).astype(np.float32)
            u += 1
    c["sbmask"] = sbm.astype(ml_dtypes.bfloat16)
    oh = np.zeros((4, 33, LG), np.float32)
    m = np.arange(LG)
    for ci in range(2):
        n = m - 2047 + qoff[ci]
        bk = t5_bucket_np(n)
        for ty in range(2):
            valid = (n >= 0) & (n < 512) if ty == 0 else (n >= 0)
            o = oh[ci * 2 + ty]
            o[bk[valid], m[valid]] = 1.0
            o[32, m[~valid]] = 1.0
    c["oh"] = oh
    t_abs = np.concatenate([qoff[0] + np.arange(512), qoff[1] + np.arange(512)])[:, None]
    jj = np.arange(32)[None, :]
    cur = t_abs // 64
    forced = (jj == 0) | (jj == cur) | (jj == cur - 1)
    allowed = (jj * 64) <= t_abs
    selmul = (allowed & ~forced).astype(np.float32)
    seladd = np.where(forced, 1e9, np.where(allowed, 0.0, -1e9)).astype(np.float32)
    c["selmul"] = selmul
    c["seladd"] = seladd
    return c


def shared_consts():
    c = {}
    k = np.arange(128)[:, None]
    mm = np.arange(128)[None, :]
    cm = np.zeros((128, 5 * 128), np.float32)
    cm[:, 0:128] = (k == mm)
    cm[:, 128:256] = (k >= mm)
    cm[:, 256:384] = (k < mm)
    cm[:, 384:512] = 1.0
    c["cmat"] = cm
    es_ = np.zeros((32, 16 * 128), np.float32)
    for kb in range(16):
        for s in range(128):
            es_[2 * kb + (1 if s >= 64 else 0), kb * 128 + s] = 1.0
    c["esel"] = es_
    cs = np.arange(127) * 16
    ss = np.arange(32) * 64
    ov = np.clip(np.minimum(cs[:, None] + 32, ss[None, :] + 64) - np.maximum(cs[:, None], ss[None, :]), 0, None).astype(np.float32) / 32
    ovp = np.zeros((128, 32), np.float32)
    ovp[:127] = ov
    c["overlap"] = ovp
    c255 = np.zeros((128, 255), np.float32)
    c255[:, 127] = 1.0
    c["c255"] = c255
    c["iota16"] = np.tile((np.arange(2048) % 16).astype(np.float32)[None, :], (128, 1))
    return c


def build_program(stop="all", dbg=None):
    nc = bass.Bass("TRN2", target_bir_lowering=False)
    es = ExitStack()

    def din(name, shape, dt=F32):
        return nc.dram_tensor(name, list(shape), dt, kind="ExternalInput").ap()

    def dscr(name, shape, dt=BF16):
        return nc.dram_tensor(name, list(shape), dt, kind="Internal")

    xa = din("xa", [S, D])
    xo = din("xo", [NOWN, D])
    memb = din("memb", [256, D])
    t5 = din("t5", [32, 8])
    g_mix = din("g_mix", [1, D]); g_mq = din("g_mq", [1, D]); g_mkv = din("g_mkv", [1, D])
    g_ffn = din("g_ffn", [1, D]); g_fin = din("g_fin", [1, D])
    w_in = din("w_in", [D, IN_COLS])
    posT = din("posT", [2, 128, 32])
    w1 = din("w1", [2, 4096, 128]); w2 = din("w2", [2, 128, 128])
    w_out = din("w_out", [D, D])
    wmq = din("wmq", [D, 512]); wmk = din("wmk", [D, 512]); wmv = din("wmv", [D, 512]); wmo = din("wmo", [512, D])
    wpq = din("wpq", [D, D])
    skT = din("skT", [16, 128, 128])
    pu = din("pu", [16384, D]); pv = din("pv", [16384, D])
    cmat = din("cmat", [128, 640]); esel_d = din("esel", [32, 2048]); ovl_d = din("overlap", [128, 32])
    c255_d = din("c255", [128, 255]); iota16_d = din("iota16", [128, 2048])
    sbmask_d = din("sbmask", [24, 128, 512], BF16)
    oh_d = din("oh", [4, 33, LG])
    selmul_d = din("selmul", [NOWN, 32]); seladd_d = din("seladd", [NOWN, 32])
    y = nc.dram_tensor("y", [NOWN, D], F32, kind="ExternalOutput").ap()
    dbg_t = None
    if dbg is not None:
        dbg_t = nc.dram_tensor("dbg", list(dbg), F32, kind="ExternalOutput").ap()

    QNd = dscr("QNd", [1024, NOWN]).ap(); QSd = dscr("QSd", [1024, NOWN]).ap()
    KNd = dscr("KNd", [1024, S]).ap(); KSd = dscr("KSd", [1024, S]).ap()
    VNd = dscr("VNd", [S, 512]).ap(); VSd = dscr("VSd", [S, 1024]).ap()
    Gd = dscr("Gd", [4, 8, LG])
    Bw = [[dscr(f"Bw{ci}_{h}", [128 * (PT + 1)]) for h in range(8)] for ci in range(2)]
    Bs = [[dscr(f"Bs{ci}_{h}", [128 * (PT + 1)]) for h in range(8)] for ci in range(2)]
    Bc = [[dscr(f"Bc{ci}_{h}", [128 * (PTC + 16)]) for h in range(8)] for ci in range(2)]
    pu16 = dscr("pu16", [16384, D]).ap(); pv16 = dscr("pv16", [16384, D]).ap()
    tabB = [Buf(f"tab{i}") for i in range(1024)]
    scrB = Buf("scr")
    qkvB = {k: Buf(k) for k in ("QN", "QS", "KN", "KS", "VN", "VS")}

    sch = Sched(nc, es)
    op = sch.op
    dma = sch.dma
    conv_state = [0]
    NCH = 1024

    def conv_step(n=1, dep=None):
        for _ in range(n):
            i = conv_state[0]
            if i >= NCH:
                return
            conv_state[0] += 1
            src, dst = (pu, pu16) if i < NCH // 2 else (pv, pv16)
            r0 = (i % (NCH // 2)) * 32
            dma("pool", lambda e: e.dma_start(out=dst[r0:r0 + 32, :], in_=src[r0:r0 + 32, :]), reads=(dep or []), writes=[tabB[i]], conv=True)

    def sb(name, shape, dt, nparts=1, stack=None):
        t = (stack or es).enter_context(nc.sbuf_tensor("s_" + name, list(shape), dt))
        return T(t, nparts, name)

    PA = es.enter_context(nc.psum_tensor("PA", [128, 2048], F32))
    PB = es.enter_context(nc.psum_tensor("PB", [128, 2048], F32))
    bankB = [Buf(f"bank{i}") for i in range(8)]

    def bank(i):
        t = PA if i < 4 else PB
        k = i % 4
        return t[:, k * 512:(k + 1) * 512]

    cm_f = sb("cm_f", [128, 640], F32)
    cm_b = sb("cm_b", [128, 512], BF16)
    dma("sp", lambda e: e.dma_start(out=cm_f[:], in_=cmat[:, :]), writes=cm_f.b)
    op("dve", lambda e: e.tensor_copy(out=cm_b[:], in_=cm_f[:, 0:512]), reads=cm_f.b, writes=cm_b.b)
    identf = cm_f[:, 0:128]
    identb = cm_b[:, 0:128]
    trib = cm_b[:, 128:256]
    ntrib = cm_b[:, 256:384]
    onesb = cm_b[:, 384:512]
    CB = cm_f.b + cm_b.b

    evac_rr = [0]

    def evac(out_ap, in_ap, reads, writes, scale=None):
        evac_rr[0] ^= 1
        if evac_rr[0] or scale is not None:
            if scale is None:
                return op("act", lambda e: e.activation(out=out_ap, in_=in_ap, func=AF.Copy), reads=reads, writes=writes)
            return op("act", lambda e: e.activation(out=out_ap, in_=in_ap, func=AF.Copy, scale=scale), reads=reads, writes=writes)
        return op("dve", lambda e: e.tensor_copy(out=out_ap, in_=in_ap), reads=reads, writes=writes)

    scr_parts = []
    def init_scratch(stk):
        tblx = sb("tblx", [33, 8], F32, 1, stk)
        ohs2 = [sb(f"ohs{i}", [33, 512], F32, 1, stk) for i in range(2)]
        Gs2 = [sb(f"Gs{i}", [8, 512], BF16, 1, stk) for i in range(2)]
        GdB = [Buf(f"Gd{k}") for k in range(4)]
        dma("sp", lambda e: e.dma_start(out=tblx[0:32, :], in_=t5[:, :]), writes=tblx.b)
        op("act", lambda e: e.activation(out=tblx[0:32, :], in_=tblx[0:32, :], func=AF.Copy, scale=1.0 / SCALE), reads=tblx.b, writes=tblx.b)
        op("dve", lambda e: e.memset(tblx[32:33, :], NEG), writes=tblx.b)
        scr_all = [(k, n) for k in range(4) for n in range(5)]
        scr_pos = [0]

        def scr_load(i):
            if i < len(scr_all):
                k, n = scr_all[i]
                ohs = ohs2[i % 2]
                dma("sp", lambda e: e.dma_start(out=ohs[:], in_=oh_d[k][:, n * 512:(n + 1) * 512]), writes=ohs.b)

        scr_load(0)

        def scratch_step():
            i = scr_pos[0]
            if i >= len(scr_all):
                return
            scr_pos[0] += 1
            scr_load(i + 1)
            k, n = scr_all[i]
            ci, ty = k // 2, k % 2
            ohs = ohs2[i % 2]; Gs = Gs2[i % 2]
            op("pe", lambda e: e.matmul(bank(7)[0:8, :], tblx[0:33, :], ohs[0:33, :], start=True, stop=True), reads=tblx.b + ohs.b, writes=[bankB[7]])
            op("act", lambda e: e.activation(out=Gs[0:8, :], in_=bank(7)[0:8, :], func=AF.Copy), reads=[bankB[7]], writes=Gs.b)
            dma("act", lambda e: e.dma_start(out=Gd.ap()[k][:, n * 512:(n + 1) * 512], in_=Gs[:]), reads=Gs.b, writes=[GdB[k]])
            if n == 4:
                for h in range(8):
                    src = bass.AP(Gd, (k * 8 + h) * LG, [[0, 128], [1, LG]])
                    B = (Bw if ty == 0 else Bs)[ci][h]
                    bB = Buf("scrh")
                    scr_parts.append(bB)
                    dma("pool", lambda e: e.dma_start(out=bass.AP(B, 0, [[PT + 1, 128], [1, LG]]), in_=src), reads=[GdB[k]], writes=[bB])
                    if ty == 1:
                        bB2 = Buf("scrc")
                        scr_parts.append(bB2)
                        dma("pool", lambda e: e.dma_start(out=bass.AP(Bc[ci][h], 0, [[PTC + 16, 128], [1, LG]]), in_=src), reads=[GdB[k]], writes=[bB2])
        return scratch_step

    glT = sb("glT", [24, NOWN], F32)
    hres = sb("hres", [128, 8, D], F32, 8)
    ph1 = ExitStack()
    xaT = sb("xaT", [128, 16, S], BF16, 1, ph1)
    xoT = sb("xoT", [128, 16, NOWN], BF16, 1, ph1)

    def load_gain(name, g_ap, stack):
        gt = sb(name, [128, D], F32, 1, stack)
        src = bass.AP(g_ap.tensor, 0, [[0, 128], [1, D]])
        dma("sp", lambda e: e.dma_start(out=gt[:], in_=src), writes=gt.b)
        return gt

    def rms_tile(xt, gt, xn_out, stack_tiles):
        junk, ssq, rs = stack_tiles
        op("act", lambda e: e.activation(out=junk[:], in_=xt[:], func=AF.Square, accum_out=ssq[:, 0:1]),
           reads=xt.b, writes=junk.b + ssq.b)
        op("dve", lambda e: e.tensor_scalar(out=rs[:, 0:1], in0=ssq[:, 0:1], scalar1=1.0 / D, scalar2=EPS, op0=ALU.mult, op1=ALU.add),
           reads=ssq.b, writes=rs.b)
        op("act", lambda e: e.activation(out=rs[:, 0:1], in_=rs[:, 0:1], func=AF.Sqrt), reads=rs.b, writes=rs.b)
        op("dve", lambda e: e.reciprocal(out=rs[:, 0:1], in_=rs[:, 0:1]), reads=rs.b, writes=rs.b)
        op("dve", lambda e: e.scalar_tensor_tensor(out=xn_out[:], in0=xt[:], scalar=rs[:, 0:1], in1=gt[:], op0=ALU.mult, op1=ALU.mult),
           reads=xt.b + rs.b + gt.b, writes=xn_out.b)

    PAb = PA[:].bitcast(BF16)
    PBb = PB[:].bitcast(BF16)

    def transpose_tile(xn, dstT, col0, pview, pbanks):
        for c in range(16):
            op("pe", lambda e, c=c: e.transpose(out=pview[:, c * 128:(c + 1) * 128], in_=xn[:, c * 128:(c + 1) * 128], identity=identb),
               reads=xn.b + CB, writes=pbanks)
        src = pview[:, 0:2048].rearrange("p (c t) -> p c t", c=16)
        op("act", lambda e: e.activation(out=dstT[:, 0:8, col0:col0 + 128], in_=src[:, 0:8, :], func=AF.Copy), reads=pbanks, writes=dstT.b)
        op("dve", lambda e: e.tensor_copy(out=dstT[:, 8:16, col0:col0 + 128], in_=src[:, 8:16, :]), reads=pbanks, writes=dstT.b)

    with ExitStack() as st:
        gmix = load_gain("gmix", g_mix, st)
        scratch_step = init_scratch(st)
        xts = [sb(f"xt{i}", [128, D], F32, 1, st) for i in range(2)]
        xns = [sb(f"xn{i}", [128, D], BF16, 1, st) for i in range(2)]
        ssqs = [sb(f"ssq{i}", [128, 1], F32, 1, st) for i in range(2)]
        rss = [sb(f"rs{i}", [128, 1], F32, 1, st) for i in range(2)]
        n = 0
        for src, dstT, ntile in ((xa, xaT, 16), (xo, xoT, 8)):
            for tt in range(ntile):
                xt = xts[n % 2]; xn = xns[n % 2]
                dma("sp", lambda e, xt=xt, src=src, tt=tt: e.dma_start(out=xt[:], in_=src[tt * 128:(tt + 1) * 128, :]), writes=xt.b)
                rms_tile(xt, gmix, xn, (xn, ssqs[n % 2], rss[n % 2]))
                if n % 2 == 0:
                    transpose_tile(xn, dstT, tt * 128, PAb, bankB[0:2])
                else:
                    transpose_tile(xn, dstT, tt * 128, PBb, bankB[4:6])
                n += 1
                scratch_step()
        sch.barrier()

    w_in_v = w_in.rearrange("(c p) n -> p c n", p=128)
    with ExitStack() as st:
        wb = [sb(f"wb{i}", [128, 16, 128], BF16, 1, st) for i in range(3)]
        wbig = [sb(f"wbig{i}", [128, 16, 512], BF16, 1, st) for i in range(1)]
        stg = [sb(f"stg{i}", [128, 512], BF16, 1, st) for i in range(4)]
        nst = [0]
        nbk = [0]

        def next_bank():
            nbk[0] = (nbk[0] + 1) % 8
            return nbk[0]

        def fm_block(col0, srcT, ntc, dst, row0, key, nw):
            w = wb[nw % 3]
            dma("pool", lambda e: e.dma_start(out=w[:], in_=w_in_v[:, :, col0:col0 + 128]), writes=w.b)
            for tc in range(ntc):
                bi = next_bank()
                for c in range(16):
                    op("pe", lambda e, c=c, bi=bi, tc=tc: e.matmul(bank(bi), w[:, c, :], srcT[:, c, tc * 512:(tc + 1) * 512], start=(c == 0), stop=(c == 15)),
                       reads=w.b + srcT.b, writes=[bankB[bi]])
                s_ = stg[nst[0] % 4]; nst[0] += 1
                evac(s_[:], bank(bi), [bankB[bi]], s_.b)
                dma("sp", lambda e, s_=s_, tc=tc: e.dma_start(out=dst[row0:row0 + 128, tc * 512:(tc + 1) * 512], in_=s_[:]), reads=s_.b, writes=[qkvB[key]])


        nw = 0
        for ty, cbase in enumerate((C_KC, C_VC, C_KS, C_KW)):
            for g in range(2):
                fm_block(cbase + 128 * g, xaT, 4, KNd, (ty * 2 + g) * 128, "KN", nw); nw += 1
        for h in range(8):
            fm_block(C_KSB + 128 * h, xaT, 4, KSd, h * 128, "KS", nw); nw += 1
        for h in range(8):
            fm_block(C_QN + 128 * h, xoT, 2, QNd, h * 128, "QN", nw); nw += 1
        for h in range(8):
            fm_block(C_QS + 128 * h, xoT, 2, QSd, h * 128, "QS", nw); nw += 1
        wg = wb[nw % 3]; nw += 1
        dma("pool", lambda e: e.dma_start(out=wg[:, :, 0:24], in_=w_in_v[:, :, C_G:C_G + 24]), writes=wg.b)
        for tc in range(2):
            bi = next_bank()
            for c in range(16):
                op("pe", lambda e, c=c, bi=bi, tc=tc: e.matmul(bank(bi)[0:24, :], wg[:, c, 0:24], xoT[:, c, tc * 512:(tc + 1) * 512], start=(c == 0), stop=(c == 15)),
                   reads=wg.b + xoT.b, writes=[bankB[bi]])
            op("act", lambda e, bi=bi, tc=tc: e.activation(out=glT[0:24, tc * 512:(tc + 1) * 512], in_=bank(bi)[0:24, :], func=AF.Exp, scale=-1.0),
               reads=[bankB[bi]], writes=glT.b)
        op("act", lambda e: e.activation(out=glT[0:24, :], in_=glT[0:24, :], func=AF.Ln, bias=1.0), reads=glT.b, writes=glT.b)
        passes = [(VNd, "VN", [(C_VS, 256), (C_VW, 256)], 0), (VSd, "VS", [(C_VSB, 512)], 0), (VSd, "VS", [(C_VSB + 512, 512)], 512)]
        for pi, (dst, key, segs, dcol) in enumerate(passes):
            w = wbig[0]
            o = 0
            for (c0, nn) in segs:
                dma("pool", lambda e, o=o, c0=c0, nn=nn: e.dma_start(out=w[:, :, o:o + nn], in_=w_in_v[:, :, c0:c0 + nn]), writes=w.b)
                o += nn
            for stile in range(16):
                bi = next_bank()
                for c in range(16):
                    op("pe", lambda e, c=c, bi=bi, stile=stile: e.matmul(bank(bi), xaT[:, c, stile * 128:(stile + 1) * 128], w[:, c, :], start=(c == 0), stop=(c == 15)),
                       reads=w.b + xaT.b, writes=[bankB[bi]])
                s_ = stg[nst[0] % 4]; nst[0] += 1
                evac(s_[:], bank(bi), [bankB[bi]], s_.b)
                dma("sp", lambda e, s_=s_, stile=stile, dst=dst, dcol=dcol: e.dma_start(out=dst[stile * 128:(stile + 1) * 128, dcol:dcol + 512], in_=s_[:]),
                    reads=s_.b, writes=[qkvB[key]])
        sch.barrier()
    ph1.close()

    state = dict(nc=nc, es=es, sch=sch, op=op, dma=dma, sb=sb, bank=bank, bankB=bankB, evac=evac, PA=PA, PB=PB, PAb=PAb, PBb=PBb,
                 identf=identf, identb=identb, trib=trib, ntrib=ntrib, onesb=onesb, CB=CB, cm_f=cm_f, glT=glT)

    ph3 = ExitStack()
    oT = sb("oT", [128, 16, NOWN], BF16, 16, ph3)

    def sb_attention():
        with ExitStack() as st:
            masks = sb("sbm", [128, 24, 512], BF16, 1, st)
            dma("sp", lambda e: e.dma_start(out=masks[:], in_=sbmask_d.rearrange("u p t -> p u t")), writes=masks.b)
            k_ = sb("sbk", [128, 4, S], BF16, 1, st)
            q_ = sb("sbq", [128, 4, NOWN], BF16, 1, st)
            v_ = sb("sbv", [128, 16, 512], BF16, 1, st)
            NB = 6
            ez = [sb(f"ez{i}", [128, 512], F32, 1, st) for i in range(NB)]
            lpb = [sb(f"lpb{i}", [128, 512], BF16, 1, st) for i in range(NB)]
            ew = [sb(f"ew{i}", [128, 512], F32, 1, st) for i in range(NB)]
            wt = [sb(f"wt{i}", [128, 512], BF16, 1, st) for i in range(NB)]
            u = 0
            for stage in range(2):
                h0 = stage * 4
                dma("sp", lambda e: e.dma_start(out=k_[:], in_=KSd[h0 * 128:(h0 + 4) * 128, :].rearrange("(h d) s -> d h s", d=128)), reads=[qkvB["KS"]], writes=k_.b)
                dma("sp", lambda e: e.dma_start(out=q_[:], in_=QSd[h0 * 128:(h0 + 4) * 128, :].rearrange("(h d) s -> d h s", d=128)), reads=[qkvB["QS"]], writes=q_.b)
                dma("sp", lambda e: e.dma_start(out=v_[:], in_=VSd[:, h0 * 128:(h0 + 4) * 128].rearrange("(t p) c -> p t c", p=128)), reads=[qkvB["VS"]], writes=v_.b)
                chains = [(j, 1) for j in range(4)] + [(j, 0) for j in range(4)]
                for g0 in range(0, 8, 3):
                    grp = chains[g0:g0 + 3]
                    lens = [8 if ci == 0 else 16 for (_, ci) in grp]
                    def fa(act_):
                        for c in act_:
                            op("pe", lambda e: e.matmul(bank(c["bZ"]), k_[:, c["j"], c["kb"] * 128:(c["kb"] + 1) * 128], q_[:, c["j"], c["ci"] * 512:(c["ci"] + 1) * 512], start=True, stop=True),
                               reads=k_.b + q_.b, writes=[bankB[c["bZ"]]])
                            op("act", lambda e: e.activation(out=ez[c["x"]][:], in_=bank(c["bZ"]), func=AF.Exp, scale=SCALE), reads=[bankB[c["bZ"]]], writes=ez[c["x"]].b)

                    def fb(act_):
                        for c in act_:
                            x = c["x"]
                            if c["general"]:
                                op("dve", lambda e: e.tensor_tensor(out=ez[x][:], in0=ez[x][:], in1=masks[:, c["mi"], :], op=ALU.mult), reads=ez[x].b + masks.b, writes=ez[x].b)
                            op("act", lambda e: e.activation(out=lpb[x][:], in_=ez[x][:], func=AF.Ln, bias=1.0), reads=ez[x].b, writes=lpb[x].b)

                    def b1(act_):
                        for c in act_:
                            x = c["x"]; bA = c["bA"]
                            op("pe", lambda e: e.matmul(bank(bA), trib, lpb[x][:], start=c["first"], stop=True, skip_group_check=True), reads=lpb[x].b + CB, writes=[bankB[bA]])
                            op("act", lambda e: e.activation(out=ew[x][:], in_=bank(bA), func=AF.Exp, scale=-1.0), reads=[bankB[bA]], writes=ew[x].b)

                    def b2(act_):
                        for c in act_:
                            x = c["x"]; bA = c["bA"]
                            if c["kb"] > 0:
                                op("pe", lambda e: e.matmul(bank(bA), ntrib, lpb[x][:], start=False, stop=True, skip_group_check=True), reads=lpb[x].b + CB, writes=[bankB[bA]])

                    def b3(act_):
                        for c in act_:
                            x = c["x"]; bO = c["bO"]
                            op("dve", lambda e: e.tensor_tensor(out=wt[x][:], in0=ez[x][:], in1=ew[x][:], op=ALU.mult), reads=ez[x].b + ew[x].b, writes=wt[x].b)
                            op("pe", lambda e: e.matmul(bank(bO), v_[:, c["kb"], c["j"] * 128:(c["j"] + 1) * 128], wt[x][:], start=c["first"], stop=(c["kb"] == 0)),
                               reads=v_.b + wt[x].b, writes=[bankB[bO]])
                            if c["kb"] == 0:
                                hh = 8 + h0 + c["j"]
                                evac(oT[:, hh, c["ci"] * 512:(c["ci"] + 1) * 512], bank(bO), [bankB[bO]], [oT.b[hh]])
                        conv_step(5, dep=wt[act_[-1]["x"]].b)

                    prev = None
                    for s_ in range(max(lens)):
                        act_ = []
                        for i, (j, ci) in enumerate(grp):
                            if s_ >= lens[i]:
                                continue
                            kb = lens[i] - 1 - s_
                            mi = kb if ci == 0 else 8 + kb
                            general = (ci == 0) or (kb >= 8)
                            act_.append(dict(i=i, j=j, ci=ci, kb=kb, mi=mi, general=general, first=(s_ == 0), bA=2 + i, bO=5 + i, x=u % NB, bZ=u % 2))
                            u += 1
                        if prev is not None:
                            b1(prev)
                        fa(act_)
                        if prev is not None:
                            b2(prev)
                            b3(prev)
                        fb(act_)
                        prev = act_
                    b1(prev); b2(prev); b3(prev)
            sch.barrier()

    def bc(ap, pattern, off=0):
        return bass.AP(ap.tensor, ap.offset + off, [list(ap.ap[0])] + [list(p) for p in pattern])


    def convert_tables():
        for i in range(16):
            src, dst = (pu, pu16) if i < 8 else (pv, pv16)
            r0 = (i % 8) * 2048
            dma("pool", lambda e: e.dma_start(out=dst[r0:r0 + 2048, :], in_=src[r0:r0 + 2048, :]), writes=[tabB[i]])

    def nsa_attention():
        with ExitStack() as st:
            esel = sb("esel", [32, 2048], BF16, 1, st)
            ovl = sb("ovl", [128, 32], F32, 1, st)
            smul = sb("smul", [128, 8, 32], F32, 1, st)
            sadd = sb("sadd", [128, 8, 32], F32, 1, st)
            dma("pool", lambda e: e.dma_start(out=esel[:], in_=esel_d[:, :]), writes=esel.b)
            dma("sp", lambda e: e.dma_start(out=ovl[:], in_=ovl_d[:, :]), writes=ovl.b)
            dma("sp", lambda e: e.dma_start(out=smul[:], in_=selmul_d.rearrange("(t p) j -> p t j", p=128)), writes=smul.b)
            dma("sp", lambda e: e.dma_start(out=sadd[:], in_=seladd_d.rearrange("(t p) j -> p t j", p=128)), writes=sadd.b)
            w1s = sb("w1s", [128, 32, 128], BF16, 1, st)
            w2s = sb("w2s", [128, 128], BF16, 1, st)
            posTs = sb("posTs", [128, 32], BF16, 1, st)
            kT = sb("nk", [128, 4, S], BF16, 4, st)
            qT = sb("nq", [128, 4, NOWN], BF16, 1, st)
            v2 = sb("nv", [128, 16, 256], BF16, 1, st)
            kccT = sb("kccT", [128, 128], BF16, 1, st)
            vcc = sb("vcc", [128, 128], BF16, 1, st)
            b1s = sb("b1s", [128, 1], F32, 1, st)
            hT = sb("hT", [128, 128], BF16, 1, st)
            oaccC = sb("oaccC", [128, 4, 512], BF16, 4, st)
            negselT = sb("negselT", [32, 512], BF16, 1, st)
            impT = sb("impT", [32, 512], F32, 1, st)
            e32 = [sb(f"e32_{i}", [128, 512], F32, 1, st) for i in range(2)]
            ebf = [sb(f"ebf{i}", [128, 512], BF16, 1, st) for i in range(5)]
            biasb = [sb(f"biasb{i}", [128, 512], BF16, 1, st) for i in range(3)]
            wides = [sb(f"wide{i}", [128, 2432], BF16, 1, st) for i in range(2)]

            def kbs_of(ci_, br_):
                if br_ == 0:
                    return list(range(0, 8)) if ci_ == 0 else list(range(4, 16))
                return list(range(0, 8)) if ci_ == 0 else list(range(0, 16))

            blist = [(g_, ci_, r_, br_) for g_ in range(2) for ci_ in range(2) for r_ in range(4) for br_ in range(2)]
            wstate = dict(next=0, cur=0)

            def issue_wide(idx):
                g_, ci_, r_, br_ = blist[idx]
                kb_ = kbs_of(ci_, br_)
                W = 512 + 128 * (kb_[-1] - kb_[0])
                Bh_ = (Bw if br_ == 0 else Bs)[ci_][4 * g_ + r_]
                wd = wides[idx % 2]
                dma("sp", lambda e: e.dma_start(out=wd[:, 0:W], in_=bass.AP(Bh_, 2047 - 128 * kb_[-1], [[PT, 128], [1, W]])), reads=scr_parts, writes=wd.b)
            dens = sb("dens", [128, 512], F32, 1, st)
            lsb = [sb(f"lsb{i}", [128, 512], F32, 1, st) for i in range(4)]
            l2 = sb("l2", [128, 512], F32, 1, st)
            rden = sb("rden", [128, 512], F32, 1, st)
            tmpg = sb("tmpg", [128, 512], F32, 1, st)
            Sg = sb("Sg", [128, 512], F32, 1, st)
            p32 = sb("p32", [128, 512], F32, 1, st)
            acc = sb("acc", [128, 512], F32, 1, st)
            tmp2 = sb("tmp2", [128, 512], F32, 1, st)
            sc = sb("sc", [128, 32], F32, 1, st)
            wks = sb("wks", [128, 32], F32, 1, st)
            v8a = sb("v8a", [128, 8], F32, 1, st)
            v8b = sb("v8b", [128, 8], F32, 1, st)
            nsl = sb("nsl", [128, 32], F32, 1, st)
            cnt = dict(L=0, d=0, o=0, e=0, b=0)

            def finalize(bd, bo, gcol, ci, out_ap, out_bufs, want_rden=False):
                op("dve", lambda e: e.tensor_scalar_max(out=dens[:], in0=bank(bd), scalar1=1e-18), reads=[bankB[bd]], writes=dens.b)
                op("act", lambda e: e.activation(out=l2[:], in_=dens[:], func=AF.Ln), reads=dens.b, writes=l2.b)
                if want_rden:
                    op("act", lambda e: e.activation(out=rden[:], in_=l2[:], func=AF.Exp, scale=-1.0), reads=l2.b, writes=rden.b)
                op("pe", lambda e: e.matmul(bank(7), identf[0:24, gcol:gcol + 1].to_broadcast([24, 128]), glT[0:24, ci * 512:(ci + 1) * 512], start=True, stop=True),
                   reads=CB + glT.b, writes=[bankB[7]])
                op("dve", lambda e: e.tensor_tensor(out=tmpg[:], in0=l2[:], in1=bank(7), op=ALU.add), reads=l2.b + [bankB[7]], writes=tmpg.b)
                op("act", lambda e: e.activation(out=Sg[:], in_=tmpg[:], func=AF.Exp, scale=-1.0), reads=tmpg.b, writes=Sg.b)
                op("dve", lambda e: e.tensor_tensor(out=out_ap, in0=bank(bo), in1=Sg[:], op=ALU.mult), reads=[bankB[bo]] + Sg.b, writes=out_bufs)

            fq = []

            def fin_gen(bd, bo, gcol, ci, out_ap, out_bufs, post):
                yield op("dve", lambda e: e.tensor_scalar_max(out=dens[:], in0=bank(bd), scalar1=1e-18), reads=[bankB[bd]], writes=dens.b)
                yield op("act", lambda e: e.activation(out=l2[:], in_=dens[:], func=AF.Ln), reads=dens.b, writes=l2.b)
                yield op("pe", lambda e: e.matmul(bank(7), identf[0:24, gcol:gcol + 1].to_broadcast([24, 128]), glT[0:24, ci * 512:(ci + 1) * 512], start=True, stop=True),
                         reads=CB + glT.b, writes=[bankB[7]])
                yield op("dve", lambda e: e.tensor_tensor(out=tmpg[:], in0=l2[:], in1=bank(7), op=ALU.add), reads=l2.b + [bankB[7]], writes=tmpg.b)
                yield op("act", lambda e: e.activation(out=Sg[:], in_=tmpg[:], func=AF.Exp, scale=-1.0), reads=tmpg.b, writes=Sg.b)
                yield op("dve", lambda e: e.tensor_tensor(out=out_ap, in0=bank(bo), in1=Sg[:], op=ALU.mult), reads=[bankB[bo]] + Sg.b, writes=out_bufs)
                for p_ in post:
                    yield p_()

            def pump(n=1):
                for _ in range(n):
                    advanced = False
                    while fq and not advanced:
                        try:
                            next(fq[0])
                            advanced = True
                        except StopIteration:
                            fq.pop(0)
                    if not advanced:
                        return

            def drain(keep=0):
                while len(fq) > keep:
                    try:
                        next(fq[0])
                    except StopIteration:
                        fq.pop(0)

            for g in range(2):
                for ty in range(4):
                    r0 = (ty * 2 + g) * 128
                    dma("sp", lambda e: e.dma_start(out=kT[:, ty, :], in_=KNd[r0:r0 + 128, :]), reads=[qkvB["KN"]], writes=[kT.b[ty]])
                dma("sp", lambda e: e.dma_start(out=qT[:], in_=QNd[g * 512:(g + 1) * 512, :].rearrange("(h d) s -> d h s", d=128)), reads=[qkvB["QN"]], writes=qT.b)
                dma("sp", lambda e: e.dma_start(out=v2[:, :, 0:128], in_=VNd[:, 128 * g:128 * g + 128].rearrange("(t p) c -> p t c", p=128)), reads=[qkvB["VN"]], writes=v2.b)
                dma("sp", lambda e: e.dma_start(out=v2[:, :, 128:256], in_=VNd[:, 256 + 128 * g:256 + 128 * g + 128].rearrange("(t p) c -> p t c", p=128)), reads=[qkvB["VN"]], writes=v2.b)
                for kv in range(2):
                    dma("pool", lambda e: e.dma_start(out=w1s[:], in_=w1[kv].rearrange("(l d) f -> d l f", d=128)), writes=w1s.b)
                    dma("pool", lambda e: e.dma_start(out=w2s[:], in_=w2[kv]), writes=w2s.b)
                    dma("pool", lambda e: e.dma_start(out=posTs[:], in_=posT[kv]), writes=posTs.b)
                    for l in range(32):
                        op("pe", lambda e: e.matmul(bank(6)[:, 0:1], w1s[:, l, :], posTs[:, l:l + 1], start=(l == 0), stop=(l == 31)),
                           reads=w1s.b + posTs.b, writes=[bankB[6]])
                    op("dve", lambda e: e.tensor_copy(out=b1s[:, 0:1], in_=bank(6)[:, 0:1]), reads=[bankB[6]], writes=b1s.b)
                    for l in range(32):
                        op("pe", lambda e: e.matmul(bank(7)[:, 0:127], w1s[:, l, :], kT[:, kv, l:l + 16 * 126 + 1:16], start=(l == 0), stop=(l == 31)),
                           reads=w1s.b + [kT.b[kv]], writes=[bankB[7]])
                    op("act", lambda e: e.activation(out=hT[:, 0:127], in_=bank(7)[:, 0:127], func=AF.Gelu_apprx_tanh, bias=b1s[:, 0:1]),
                       reads=[bankB[7]] + b1s.b, writes=hT.b)
                    if kv == 0:
                        op("pe", lambda e: e.matmul(bank(6)[:, 0:127], w2s[:], hT[:, 0:127], start=True, stop=True), reads=w2s.b + hT.b, writes=[bankB[6]])
                        op("dve", lambda e: e.tensor_copy(out=kccT[:, 0:127], in_=bank(6)[:, 0:127]), reads=[bankB[6]], writes=kccT.b)
                    else:
                        op("pe", lambda e: e.matmul(bank(6)[0:127, 0:128], hT[:, 0:127], w2s[:], start=True, stop=True), reads=w2s.b + hT.b, writes=[bankB[6]])
                        op("dve", lambda e: e.tensor_copy(out=vcc[0:127, :], in_=bank(6)[0:127, 0:128]), reads=[bankB[6]], writes=vcc.b)
                for ci in range(2):
                    qs = slice(ci * 512, (ci + 1) * 512)
                    drain(0)
                    for r in range(4):
                        h = 4 * g + r
                        bL = cnt["L"] % 2; cnt["L"] += 1
                        bd = 2 + cnt["d"] % 2; cnt["d"] += 1
                        bo = 4 + cnt["o"] % 2; cnt["o"] += 1
                        bt = biasb[cnt["b"] % 3]; cnt["b"] += 1
                        ee = e32[cnt["e"] % 2]; eb = ebf[cnt["e"] % 5]; cnt["e"] += 1
                        dma("sp", lambda e: e.dma_start(out=bt[0:127, :], in_=bass.AP(Bc[ci][h], 2016, [[PTC, 127], [1, 512]])), reads=scr_parts, writes=bt.b)
                        op("pe", lambda e: e.matmul(bank(bL)[0:127, :], kccT[:, 0:127], qT[:, r, qs], start=True, stop=False), reads=kccT.b + qT.b, writes=[bankB[bL]])
                        op("pe", lambda e: e.matmul(bank(bL)[0:127, :], identb[0:127, 0:127], bt[0:127, :], start=False, stop=True), reads=CB + bt.b, writes=[bankB[bL]])
                        op("act", lambda e: e.activation(out=ee[0:127, :], in_=bank(bL)[0:127, :], func=AF.Exp, scale=SCALE), reads=[bankB[bL]], writes=ee.b)
                        op("dve", lambda e: e.tensor_copy(out=eb[0:127, :], in_=ee[0:127, :]), reads=ee.b, writes=eb.b)
                        op("pe", lambda e: e.matmul(bank(bd), onesb[0:127, :], eb[0:127, :], start=True, stop=True), reads=CB + eb.b, writes=[bankB[bd]])
                        op("pe", lambda e: e.matmul(bank(bo), vcc[0:127, :], eb[0:127, :], start=True, stop=True), reads=vcc.b + eb.b, writes=[bankB[bo]])
                        finalize(bd, bo, 3 * h + 0, ci, oaccC[:, r, :], [oaccC.b[r]], want_rden=True)
                        op("dve", lambda e: e.tensor_tensor(out=p32[0:127, :], in0=ee[0:127, :], in1=rden[0:127, :], op=ALU.mult), reads=ee.b + rden.b, writes=p32.b)
                        op("pe", lambda e: e.matmul(bank(6)[0:32, :], ovl[0:127, :], p32[0:127, :], start=(r == 0), stop=(r == 3)), reads=ovl.b + p32.b, writes=[bankB[6]])
                    op("act", lambda e: e.activation(out=impT[:], in_=bank(6)[0:32, :], func=AF.Copy), reads=[bankB[6]], writes=impT.b)
                    for tt in range(4):
                        op("pe", lambda e: e.transpose(out=bank(7)[:, 0:32], in_=impT[0:32, tt * 128:(tt + 1) * 128], identity=identf[0:32, 0:32]),
                           reads=impT.b + CB, writes=[bankB[7]])
                        op("dve", lambda e: e.tensor_tensor(out=sc[:], in0=bank(7)[:, 0:32], in1=smul[:, ci * 4 + tt, :], op=ALU.mult), reads=[bankB[7]] + smul.b, writes=sc.b)
                        op("dve", lambda e: e.tensor_tensor(out=sc[:], in0=sc[:], in1=sadd[:, ci * 4 + tt, :], op=ALU.add), reads=sc.b + sadd.b, writes=sc.b)
                        op("dve", lambda e: e.max(out=v8a[:], in_=sc[:]), reads=sc.b, writes=v8a.b)
                        op("dve", lambda e: e.match_replace(out=wks[:], in_to_replace=v8a[:], in_values=sc[:], imm_value=-3e38), reads=sc.b + v8a.b, writes=wks.b)
                        op("dve", lambda e: e.max(out=v8b[:], in_=wks[:]), reads=wks.b, writes=v8b.b)
                        op("dve", lambda e: e.tensor_scalar(out=nsl[:], in0=sc[:], scalar1=v8b[:, 7:8], scalar2=-1.0, op0=ALU.is_ge, op1=ALU.add),
                           reads=sc.b + v8b.b, writes=nsl.b)
                        op("pe", lambda e: e.transpose(out=bank(7)[0:32, 128:256], in_=nsl[:, 0:32], identity=identf), reads=nsl.b + CB, writes=[bankB[7]])
                        op("act", lambda e: e.activation(out=negselT[0:32, tt * 128:(tt + 1) * 128], in_=bank(7)[0:32, 128:256], func=AF.Copy, scale=-NEG),
                           reads=[bankB[7]], writes=negselT.b)
                    for r in range(4):
                        h = 4 * g + r
                        for br in range(2):
                            if br == 0:
                                kbs = list(range(0, 8)) if ci == 0 else list(range(4, 16))
                            else:
                                kbs = list(range(0, 8)) if ci == 0 else list(range(0, 16))
                            bd = 2 + cnt["d"] % 2; cnt["d"] += 1
                            bo = 4 + cnt["o"] % 2; cnt["o"] += 1
                            Bh = (Bw if br == 0 else Bs)[ci][h]
                            kty = 3 if br == 0 else 2
                            vo = 128 if br == 0 else 0
                            drain(1)
                            bidx_ = wstate["cur"]; wstate["cur"] += 1
                            assert blist[bidx_] == (g, ci, r, br)
                            while wstate["next"] <= min(bidx_ + 1, len(blist) - 1):
                                issue_wide(wstate["next"]); wstate["next"] += 1
                            wd = wides[bidx_ % 2]
                            pending = []
                            for n, kb in enumerate(kbs):
                                bL = cnt["L"] % 2; cnt["L"] += 1
                                eb = ebf[cnt["e"] % 5]; cnt["e"] += 1
                                wo_ = (kbs[-1] - kb) * 128
                                op("pe", lambda e: e.matmul(bank(bL), kT[:, kty, kb * 128:(kb + 1) * 128], qT[:, r, qs], start=True, stop=(br == 0)),
                                   reads=[kT.b[kty]] + qT.b, writes=[bankB[bL]])
                                if br == 1:
                                    op("pe", lambda e: e.matmul(bank(bL), esel[0:32, kb * 128:(kb + 1) * 128], negselT[0:32, :], start=False, stop=True),
                                       reads=esel.b + negselT.b, writes=[bankB[bL]])
                                ls_ = lsb[cnt["L"] % 4]
                                op("dve", lambda e: e.tensor_tensor(out=ls_[:], in0=bank(bL), in1=wd[:, wo_:wo_ + 512], op=ALU.add), reads=[bankB[bL]] + wd.b, writes=ls_.b)
                                if len(pending) > 1:
                                    pending.pop(0)()
                                op("act", lambda e: e.activation(out=eb[:], in_=ls_[:], func=AF.Exp, scale=SCALE), reads=ls_.b, writes=eb.b)
                                pump(1)
                                for _d in range(1):
                                    op("pe", lambda e: e.matmul(bank(6), identb, cm_b[:, 0:512], start=True, stop=True), reads=CB, writes=[bankB[6]])
                                if n % 2 == 0:
                                    conv_step(dep=eb.b)

                                def pend(eb=eb, n=n, kb=kb):
                                    op("pe", lambda e: e.matmul(bank(bd), onesb, eb[:], start=(n == 0), stop=(n == len(kbs) - 1)), reads=CB + eb.b, writes=[bankB[bd]])
                                    op("pe", lambda e: e.matmul(bank(bo), v2[:, kb, vo:vo + 128], eb[:], start=(n == 0), stop=(n == len(kbs) - 1)),
                                       reads=v2.b + eb.b, writes=[bankB[bo]])
                                pending.append(pend)
                            while pending:
                                pending.pop(0)()
                            if br == 0:
                                fq.append(fin_gen(bd, bo, 3 * h + 2, ci, acc[:], acc.b, []))
                            else:
                                def post1():
                                    return op("dve", lambda e: e.tensor_tensor(out=acc[:], in0=acc[:], in1=tmp2[:], op=ALU.add), reads=acc.b + tmp2.b, writes=acc.b)

                                def post2(h=h, qs=qs, r=r):
                                    return op("dve", lambda e: e.tensor_tensor(out=oT[:, h, qs], in0=acc[:], in1=oaccC[:, r, :], op=ALU.add), reads=acc.b + [oaccC.b[r]], writes=[oT.b[h]])
                                fq.append(fin_gen(bd, bo, 3 * h + 1, ci, tmp2[:], tmp2.b, [post1, post2]))
            drain(0)
            sch.barrier()

    sb_attention()
    nsa_attention()

    outs = []
    if stop in ("sb", "nsa"):
        with ExitStack() as st:
            of = sb("dbg_of", [128, 8, NOWN], F32, 1, st)
            lo = 8 if stop == "sb" else 0
            op("dve", lambda e: e.tensor_copy(out=of[:], in_=oT[:, lo:lo + 8, :]), reads=oT.b, writes=of.b)
            outs.append(dma("sp", lambda e: e.dma_start(out=dbg_t.rearrange("(h d) t -> d h t", d=128), in_=of[:]), reads=of.b))
            sch.finish(outs)
        ph3.close()
        es.close()
        return nc

    def rms_ap(x_ap, x_bufs, gt, out_ap, out_bufs, junk, ssq, rs):
        op("act", lambda e: e.activation(out=junk[:], in_=x_ap, func=AF.Square, accum_out=ssq[:, 0:1]), reads=x_bufs, writes=junk.b + ssq.b)
        op("dve", lambda e: e.tensor_scalar(out=rs[:, 0:1], in0=ssq[:, 0:1], scalar1=1.0 / D, scalar2=EPS, op0=ALU.mult, op1=ALU.add), reads=ssq.b, writes=rs.b)
        op("act", lambda e: e.activation(out=rs[:, 0:1], in_=rs[:, 0:1], func=AF.Sqrt), reads=rs.b, writes=rs.b)
        op("dve", lambda e: e.reciprocal(out=rs[:, 0:1], in_=rs[:, 0:1]), reads=rs.b, writes=rs.b)
        op("dve", lambda e: e.scalar_tensor_tensor(out=out_ap, in0=x_ap, scalar=rs[:, 0:1], in1=gt[:], op0=ALU.mult, op1=ALU.mult),
           reads=x_bufs + rs.b + gt.b, writes=out_bufs)

    with ExitStack() as st:
        wo = sb("wo", [128, 16, D], BF16, 4, st)
        w_out_v = w_out.rearrange("(c p) n -> p c n", p=128)
        for nb in range(4):
            dma("pool", lambda e: e.dma_start(out=wo[:, :, nb * 512:(nb + 1) * 512], in_=w_out_v[:, :, nb * 512:(nb + 1) * 512]), writes=[wo.b[nb]])
        for tt in range(8):
            dma("sp", lambda e: e.dma_start(out=hres[:, tt, :], in_=xo[tt * 128:(tt + 1) * 128, :]), writes=[hres.b[tt]])
        nbk4 = 0
        for nb in range(4):
            for tt in range(8):
                bi = nbk4 % 8; nbk4 += 1
                for f in range(16):
                    op("pe", lambda e: e.matmul(bank(bi), oT[:, f, tt * 128:(tt + 1) * 128], wo[:, f, nb * 512:(nb + 1) * 512], start=(f == 0), stop=(f == 15)),
                       reads=[oT.b[f], wo.b[nb]], writes=[bankB[bi]])
                op("dve", lambda e: e.tensor_tensor(out=hres[:, tt, nb * 512:(nb + 1) * 512], in0=hres[:, tt, nb * 512:(nb + 1) * 512], in1=bank(bi), op=ALU.add),
                   reads=[hres.b[tt], bankB[bi]], writes=[hres.b[tt]])
                conv_step(4, dep=[hres.b[tt]])
        sch.barrier()
    ph3.close()

    if stop == "h1":
        outs = [dma("sp", lambda e: e.dma_start(out=dbg_t.rearrange("(t p) n -> p t n", p=128), in_=hres[:]), reads=hres.b)]
        sch.finish(outs)
        es.close()
        return nc

    with ExitStack() as st:
        gkv = load_gain("gkv", g_mkv, st)
        junk = sb("p5junk", [128, D], BF16, 1, st)
        ssq = sb("p5ssq", [128, 1], F32, 1, st)
        rs = sb("p5rs", [128, 1], F32, 1, st)
        xt = sb("p5xt", [128, D], F32, 1, st)
        xn = sb("p5xn", [128, D], BF16, 1, st)
        memT = sb("memT", [128, 16, 256], BF16, 1, st)
        wk_ = sb("p5wk", [128, 16, 512], BF16, 1, st)
        wv_ = sb("p5wv", [128, 16, 512], BF16, 1, st)
        wq_ = sb("p5wq", [128, 16, 512], BF16, 1, st)
        wo_ = sb("p5wo", [128, 4, D], BF16, 1, st)
        dma("pool", lambda e: e.dma_start(out=wk_[:], in_=wmk.rearrange("(c p) n -> p c n", p=128)), writes=wk_.b)
        dma("pool", lambda e: e.dma_start(out=wv_[:], in_=wmv.rearrange("(c p) n -> p c n", p=128)), writes=wv_.b)
        dma("pool", lambda e: e.dma_start(out=wq_[:], in_=wmq.rearrange("(c p) n -> p c n", p=128)), writes=wq_.b)
        dma("pool", lambda e: e.dma_start(out=wo_[:], in_=wmo.rearrange("(c p) n -> p c n", p=128)), writes=wo_.b)
        kmT = sb("kmT", [128, 4, 256], BF16, 1, st)
        vm = sb("vm", [128, 2, 512], BF16, 1, st)
        hnT = sb("p5hnT", [128, 16, 512], BF16, 1, st)
        qmT = sb("qmT", [128, 4, 512], BF16, 1, st)
        omT = sb("omT", [128, 4, 512], BF16, 1, st)
        em = [sb(f"em{i}", [128, 512], BF16, 1, st) for i in range(2)]
        l2m = sb("l2m", [128, 512], F32, 1, st)
        rdm = sb("rdm", [128, 512], F32, 1, st)
        for mt in range(2):
            dma("sp", lambda e: e.dma_start(out=xt[:], in_=memb[mt * 128:(mt + 1) * 128, :]), writes=xt.b)
            rms_ap(xt[:], xt.b, gkv, xn[:], xn.b, junk, ssq, rs)
            transpose_tile(xn, memT, mt * 128, PAb, bankB[0:2])
        gq = gkv
        dma("sp", lambda e: e.dma_start(out=gq[:], in_=bass.AP(g_mq.tensor, 0, [[0, 128], [1, D]])), writes=gq.b)
        for hh in range(4):
            for c in range(16):
                op("pe", lambda e: e.matmul(bank(4)[:, 0:256], wk_[:, c, hh * 128:(hh + 1) * 128], memT[:, c, :], start=(c == 0), stop=(c == 15)),
                   reads=wk_.b + memT.b, writes=[bankB[4]])
            evac(kmT[:, hh, :], bank(4)[:, 0:256], [bankB[4]], kmT.b)
        for mt in range(2):
            for c in range(16):
                op("pe", lambda e: e.matmul(bank(5), memT[:, c, mt * 128:(mt + 1) * 128], wv_[:, c, :], start=(c == 0), stop=(c == 15)),
                   reads=wv_.b + memT.b, writes=[bankB[5]])
            evac(vm[:, mt, :], bank(5), [bankB[5]], vm.b)
        for tc in range(2):
            for t4 in range(4):
                tt = tc * 4 + t4
                rms_ap(hres[:, tt, :], [hres.b[tt]], gq, xn[:], xn.b, junk, ssq, rs)
                transpose_tile(xn, hnT, t4 * 128, PAb, bankB[0:2])
            for hh in range(4):
                for c in range(16):
                    op("pe", lambda e: e.matmul(bank(4), wq_[:, c, hh * 128:(hh + 1) * 128], hnT[:, c, :], start=(c == 0), stop=(c == 15)),
                       reads=wq_.b + hnT.b, writes=[bankB[4]])
                evac(qmT[:, hh, :], bank(4), [bankB[4]], qmT.b)
            for hh in range(4):
                for mt in range(2):
                    bL = 2 + mt
                    op("pe", lambda e: e.matmul(bank(bL), kmT[:, hh, mt * 128:(mt + 1) * 128], qmT[:, hh, :], start=True, stop=True), reads=kmT.b + qmT.b, writes=[bankB[bL]])
                    op("act", lambda e: e.activation(out=em[mt][:], in_=bank(bL), func=AF.Exp, scale=SCALE), reads=[bankB[bL]], writes=em[mt].b)
                    op("pe", lambda e: e.matmul(bank(6), onesb, em[mt][:], start=(mt == 0), stop=(mt == 1)), reads=CB + em[mt].b, writes=[bankB[6]])
                    op("pe", lambda e: e.matmul(bank(7), vm[:, mt, hh * 128:(hh + 1) * 128], em[mt][:], start=(mt == 0), stop=(mt == 1)), reads=vm.b + em[mt].b, writes=[bankB[7]])
                op("act", lambda e: e.activation(out=l2m[:], in_=bank(6), func=AF.Ln), reads=[bankB[6]], writes=l2m.b)
                op("act", lambda e: e.activation(out=rdm[:], in_=l2m[:], func=AF.Exp, scale=-1.0), reads=l2m.b, writes=rdm.b)
                op("dve", lambda e: e.tensor_tensor(out=omT[:, hh, :], in0=bank(7), in1=rdm[:], op=ALU.mult), reads=[bankB[7]] + rdm.b, writes=omT.b)
            for t4 in range(4):
                tt = tc * 4 + t4
                for nb in range(4):
                    bi = 4 + nb
                    for f in range(4):
                        op("pe", lambda e: e.matmul(bank(bi), omT[:, f, t4 * 128:(t4 + 1) * 128], wo_[:, f, nb * 512:(nb + 1) * 512], start=(f == 0), stop=(f == 3)),
                           reads=omT.b + wo_.b, writes=[bankB[bi]])
                    op("dve", lambda e: e.tensor_tensor(out=hres[:, tt, nb * 512:(nb + 1) * 512], in0=hres[:, tt, nb * 512:(nb + 1) * 512], in1=bank(bi), op=ALU.add),
                       reads=[hres.b[tt], bankB[bi]], writes=[hres.b[tt]])
        sch.barrier()

    if stop == "h2":
        outs = [dma("sp", lambda e: e.dma_start(out=dbg_t.rearrange("(t p) n -> p t n", p=128), in_=hres[:]), reads=hres.b)]
        sch.finish(outs)
        es.close()
        return nc

    outs = []
    with ExitStack() as st6:
        expT = sb("expT", [128, 8, 128], U32, 8, st6)
        gateT = sb("gateT", [128, 8, 128], F32, 8, st6)
        gffn = load_gain("gffn", g_ffn, st6)
        junk = sb("p6junk", [128, D], BF16, 1, st6)
        ssq = sb("p6ssq", [128, 1], F32, 1, st6)
        rs = sb("p6rs", [128, 1], F32, 1, st6)
        with ExitStack() as sa:
            qpT = sb("qpT", [128, 16, NOWN], BF16, 1, sa)
            with ExitStack() as sa1:
                hnT = sb("p6hnT", [128, 16, NOWN], BF16, 1, sa1)
                xn = sb("p6xn", [128, D], BF16, 1, sa1)
                wbs = [sb(f"p6wb{i}", [128, 16, 128], BF16, 1, sa1) for i in range(3)]
                for tt in range(8):
                    rms_ap(hres[:, tt, :], [hres.b[tt]], gffn, xn[:], xn.b, junk, ssq, rs)
                    if tt % 2 == 0:
                        transpose_tile(xn, hnT, tt * 128, PAb, bankB[0:2])
                    else:
                        transpose_tile(xn, hnT, tt * 128, PBb, bankB[4:6])
                wpq_v = wpq.rearrange("(c p) n -> p c n", p=128)
                nb_ = 0
                for hp in range(16):
                    w = wbs[hp % 3]
                    dma("pool", lambda e: e.dma_start(out=w[:], in_=wpq_v[:, :, hp * 128:(hp + 1) * 128]), writes=w.b)
                    for tc in range(2):
                        bi = nb_ % 8; nb_ += 1
                        for c in range(16):
                            op("pe", lambda e: e.matmul(bank(bi), w[:, c, :], hnT[:, c, tc * 512:(tc + 1) * 512], start=(c == 0), stop=(c == 15)),
                               reads=w.b + hnT.b, writes=[bankB[bi]])
                        evac(qpT[:, hp, tc * 512:(tc + 1) * 512], bank(bi), [bankB[bi]], qpT.b)
                sch.barrier()
            skTs = sb("skTs", [128, 16, 128], BF16, 1, sa)
            dma("pool", lambda e: e.dma_start(out=skTs[:], in_=skT.rearrange("k d n -> d k n")), writes=skTs.b)
            iota16 = sb("iota16", [128, 16], F32, 1, sa)
            dma("sp", lambda e: e.dma_start(out=iota16[:], in_=iota16_d[:, 0:16]), writes=iota16.b)
            thr16 = sb("thr16", [128, 16], F32, 1, sa)
            thrb = sb("thrb", [128, 16], F32, 1, sa)
            op("dve", lambda e: e.tensor_scalar(out=thr16[:], in0=iota16[:], scalar1=16.0, scalar2=None, op0=ALU.mult), reads=iota16.b, writes=thr16.b)
            op("dve", lambda e: e.tensor_scalar(out=thrb[:], in0=iota16[:], scalar1=-16.0, scalar2=None, op0=ALU.add), reads=iota16.b, writes=thrb.b)
            s_sb = sb("s_sb", [128, 2048], F32, 1, sa)
            cand = sb("cand", [128, 2048], F32, 1, sa)
            cmpw = sb("cmpw", [128, 2048], F32, 1, sa)
            prw = sb("prw", [128, 2048], F32, 1, sa)
            tops = sb("tops", [128, 256], F32, 16, sa)
            topi = sb("topi", [128, 256], U32, 16, sa)
            topf = sb("topf", [128, 256], F32, 1, sa)
            dtop = sb("dtop", [128, 256], F32, 1, sa)
            wk1s = [sb(f"wk1_{i}", [128, 128], F32, 1, sa) for i in range(4)]
            wk2s = [sb(f"wk2_{i}", [128, 256], F32, 1, sa) for i in range(4)]
            best = sb("best", [128, 128], F32, 8, sa)
            bidx = sb("bidx", [128, 128], U32, 8, sa)
            bf = sb("bf", [128, 128], F32, 1, sa)
            negm = sb("negm", [128, 8], F32, 1, sa)
            eg = sb("eg", [128, 128], F32, 1, sa)
            zz = sb("zz", [128, 8], F32, 1, sa)
            rz = sb("rz", [128, 8], F32, 1, sa)
            gate = sb("gate", [128, 128], F32, 1, sa)
            ap1 = sb("ap1", [128, 128], F32, 1, sa)
            bq = sb("bq", [128, 128], F32, 1, sa)
            ia = sb("ia", [128, 128], F32, 1, sa)
            ib = sb("ib", [128, 128], F32, 1, sa)
            expf = sb("expf", [128, 128], F32, 1, sa)
            P4 = [[256, 8], [16, 16], [1, 16]]
            rq = []

            def rpump(n=1):
                for _ in range(n):
                    adv = False
                    while rq and not adv:
                        try:
                            next(rq[0]); adv = True
                        except StopIteration:
                            rq.pop(0)
                    if not adv:
                        return

            def rdrain():
                while rq:
                    try:
                        next(rq[0])
                    except StopIteration:
                        rq.pop(0)

            for tt in range(8):
                for hp in range(16):
                    op("pe", lambda e: e.matmul(PA[:, hp * 128:(hp + 1) * 128], qpT[:, hp, tt * 128:(tt + 1) * 128], skTs[:, hp, :], start=True, stop=True),
                       reads=qpT.b + skTs.b, writes=[bankB[hp // 4]])
                op("act", lambda e: e.activation(out=s_sb[:, 0:1024], in_=PA[:, 0:1024], func=AF.Copy), reads=bankB[0:2], writes=s_sb.b)
                op("dve", lambda e: e.tensor_copy(out=s_sb[:, 1024:2048], in_=PA[:, 1024:2048]), reads=bankB[2:4], writes=s_sb.b)
                for hp0 in range(0, 16, 4):
                    G = []
                    for hp in range(hp0, hp0 + 4):
                        G.append(dict(seg=s_sb[:, hp * 128:(hp + 1) * 128], a=tops[:, hp * 16:hp * 16 + 8], b=tops[:, hp * 16 + 8:hp * 16 + 16],
                                      ia=topi[:, hp * 16:hp * 16 + 8], ib=topi[:, hp * 16 + 8:hp * 16 + 16], wk=wk1s[hp % 4], tb=[tops.b[hp]], ti=[topi.b[hp]]))
                    for g_ in G:
                        op("dve", lambda e: e.max(out=g_["a"], in_=g_["seg"]), reads=s_sb.b, writes=g_["tb"])
                    rpump(1)
                    for g_ in G:
                        op("dve", lambda e: e.match_replace(out=g_["wk"][:], in_to_replace=g_["a"], in_values=g_["seg"], imm_value=-3e38), reads=s_sb.b + g_["tb"], writes=g_["wk"].b)
                    rpump(1)
                    for g_ in G:
                        op("dve", lambda e: e.max_index(out=g_["ia"], in_max=g_["a"], in_values=g_["seg"]), reads=s_sb.b + g_["tb"], writes=g_["ti"])
                    rpump(1)
                    for g_ in G:
                        op("dve", lambda e: e.max(out=g_["b"], in_=g_["wk"][:]), reads=g_["wk"].b, writes=g_["tb"])
                    rpump(1)
                    for g_ in G:
                        op("dve", lambda e: e.max_index(out=g_["ib"], in_max=g_["b"], in_values=g_["seg"]), reads=s_sb.b + g_["tb"], writes=g_["ti"])
                    conv_step(10, dep=G[-1]["ti"])
                op("dve", lambda e: e.tensor_tensor(out=bc(cand[:], P4), in0=bc(tops[:], [[32, 8], [1, 16], [0, 16]]), in1=bc(tops[:], [[32, 8], [0, 16], [1, 16]], 16), op=ALU.add),
                   reads=tops.b, writes=cand.b)
                for h0_ in range(0, 8, 4):
                    G = []
                    for h in range(h0_, h0_ + 4):
                        G.append(dict(seg=cand[:, h * 256:(h + 1) * 256], a=best[:, h * 16:h * 16 + 8], b=best[:, h * 16 + 8:h * 16 + 16],
                                      ia=bidx[:, h * 16:h * 16 + 8], ib=bidx[:, h * 16 + 8:h * 16 + 16], wk=wk2s[h % 4], tb=[best.b[h]], ti=[bidx.b[h]]))
                    for g_ in G:
                        op("dve", lambda e: e.max(out=g_["a"], in_=g_["seg"]), reads=cand.b, writes=g_["tb"])
                    rpump(1)
                    for g_ in G:
                        op("dve", lambda e: e.match_replace(out=g_["wk"][:], in_to_replace=g_["a"], in_values=g_["seg"], imm_value=-3e38), reads=cand.b + g_["tb"], writes=g_["wk"].b)
                    rpump(1)
                    for g_ in G:
                        op("dve", lambda e: e.max_index(out=g_["ia"], in_max=g_["a"], in_values=g_["seg"]), reads=cand.b + g_["tb"], writes=g_["ti"])
                    rpump(1)
                    for g_ in G:
                        op("dve", lambda e: e.max(out=g_["b"], in_=g_["wk"][:]), reads=g_["wk"].b, writes=g_["tb"])
                    for g_ in G:
                        op("dve", lambda e: e.max_index(out=g_["ib"], in_max=g_["b"], in_values=g_["seg"]), reads=cand.b + g_["tb"], writes=g_["ti"])
                rdrain()
                op("dve", lambda e: e.tensor_scalar(out=negm[:], in0=bc(best[:], [[16, 8]]), scalar1=-1.0, scalar2=None, op0=ALU.mult), reads=best.b, writes=negm.b)
                for h in range(8):
                    op("act", lambda e: e.activation(out=eg[:, h * 16:(h + 1) * 16], in_=best[:, h * 16:(h + 1) * 16], func=AF.Exp, bias=negm[:, h:h + 1], accum_out=zz[:, h:h + 1]),
                       reads=best.b + negm.b, writes=eg.b + zz.b)
                op("dve", lambda e: e.tensor_copy(out=bf[:], in_=bidx[:]), reads=bidx.b, writes=bf.b)
                op("dve", lambda e: e.tensor_copy(out=topf[:], in_=topi[:]), reads=topi.b, writes=topf.b)
                def routeB(tt):
                    yield op("dve", lambda e: e.reciprocal(out=rz[:], in_=zz[:]), reads=zz.b, writes=rz.b)
                    yield op("dve", lambda e: e.tensor_tensor(out=bc(gate[:], [[16, 8], [1, 16]]), in0=bc(eg[:], [[16, 8], [1, 16]]), in1=bc(rz[:], [[1, 8], [0, 16]]), op=ALU.mult),
                       reads=eg.b + rz.b, writes=gate.b)
                    yield op("dve", lambda e: e.tensor_copy(out=dtop[:], in_=topf[:]), reads=topf.b, writes=dtop.b)
                    yield op("dve", lambda e: e.tensor_tensor(out=bc(dtop[:], [[16, 16], [1, 15]], 1), in0=bc(topf[:], [[16, 16], [1, 15]], 1), in1=bc(topf[:], [[16, 16], [1, 15]], 0), op=ALU.subtract),
                       reads=topf.b + dtop.b, writes=dtop.b)
                    yield op("dve", lambda e: e.tensor_tensor(out=bc(cmpw[:], P4), in0=bc(bf[:], [[16, 8], [1, 16], [0, 16]]), in1=bc(thr16[:], [[0, 8], [0, 16], [1, 16]]), op=ALU.is_ge),
                       reads=bf.b + thr16.b, writes=cmpw.b)
                    yield op("dve", lambda e: e.tensor_tensor(out=bc(prw[:], P4), in0=bc(cmpw[:], P4), in1=bc(dtop[:], [[32, 8], [0, 16], [1, 16]]), op=ALU.mult),
                       reads=cmpw.b + dtop.b, writes=prw.b)
                    yield op("dve", lambda e: e.tensor_reduce(out=ia[:], in_=bc(prw[:], [[16, 128], [1, 16]]), axis=AX.X, op=ALU.add), reads=prw.b, writes=ia.b)
                    yield op("dve", lambda e: e.tensor_reduce(out=ap1[:], in_=bc(cmpw[:], [[16, 128], [1, 16]]), axis=AX.X, op=ALU.add), reads=cmpw.b, writes=ap1.b)
                    yield op("dve", lambda e: e.scalar_tensor_tensor(out=bq[:], in0=ap1[:], scalar=-16.0, in1=bf[:], op0=ALU.mult, op1=ALU.add), reads=ap1.b + bf.b, writes=bq.b)
                    yield op("dve", lambda e: e.tensor_tensor(out=bc(cmpw[:], P4), in0=bc(bq[:], [[16, 8], [1, 16], [0, 16]]), in1=bc(thrb[:], [[0, 8], [0, 16], [1, 16]]), op=ALU.is_ge),
                       reads=bq.b + thrb.b, writes=cmpw.b)
                    yield op("dve", lambda e: e.tensor_tensor(out=bc(prw[:], P4), in0=bc(cmpw[:], P4), in1=bc(dtop[:], [[32, 8], [0, 16], [1, 16]], 16), op=ALU.mult),
                       reads=cmpw.b + dtop.b, writes=prw.b)
                    yield op("dve", lambda e: e.tensor_reduce(out=ib[:], in_=bc(prw[:], [[16, 128], [1, 16]]), axis=AX.X, op=ALU.add), reads=prw.b, writes=ib.b)
                    yield op("dve", lambda e: e.scalar_tensor_tensor(out=expf[:], in0=ia[:], scalar=128.0, in1=ib[:], op0=ALU.mult, op1=ALU.add), reads=ia.b + ib.b, writes=expf.b)
                    yield op("pe", lambda e: e.transpose(out=bank(7)[:, 0:128], in_=expf[:], identity=identf), reads=expf.b + CB, writes=[bankB[7]])
                    yield op("dve", lambda e: e.tensor_copy(out=expT[:, tt, :], in_=bank(7)[:, 0:128]), reads=[bankB[7]], writes=[expT.b[tt]])
                    yield op("pe", lambda e: e.transpose(out=bank(6)[:, 0:128], in_=gate[:], identity=identf), reads=gate.b + CB, writes=[bankB[6]])
                    yield op("act", lambda e: e.activation(out=gateT[:, tt, :], in_=bank(6)[:, 0:128], func=AF.Copy), reads=[bankB[6]], writes=[gateT.b[tt]])
                rq.append(routeB(tt))
            rdrain()
            sch.barrier()

        if stop == "route":
            with ExitStack() as sd:
                of = sb("dbg_of", [128, 8, 256], F32, 1, sd)
                op("dve", lambda e: e.tensor_copy(out=of[:, :, 0:128], in_=expT[:]), reads=expT.b, writes=of.b)
                op("dve", lambda e: e.tensor_copy(out=of[:, :, 128:256], in_=gateT[:]), reads=gateT.b, writes=of.b)
                outs.append(dma("sp", lambda e: e.dma_start(out=dbg_t.rearrange("p (a b) -> p a b", a=8), in_=of[:]), reads=of.b))
                sch.finish(outs)
            es.close()
            return nc

        with ExitStack() as sg:
            gfin = load_gain("gfin", g_fin, sg)
            c255 = sb("c255", [128, 255], F32, 1, sg)
            dma("sp", lambda e: e.dma_start(out=c255[:], in_=c255_d[:, :]), writes=c255.b)
            hds = [sb(f"hd{i}", [128, 1], F32, 1, sg) for i in range(4)]
            xn3 = sb("xn3", [128, D], BF16, 1, sg)
            NG = 8
            conv_step(NCH)
            Ug = [sb(f"Ug{i}", [128, D], BF16, 1, sg) for i in range(NG)]
            NGV = 12
            Vg = [sb(f"Vg{i}", [128, D], BF16, 1, sg) for i in range(NGV)]
            aT = [sb(f"aT{i}", [128, 1], F32, 1, sg) for i in range(8)]
            ga = [sb(f"ga{i}", [128, 1], F32, 1, sg) for i in range(8)]
            Lt = [sb(f"Lt{i}", [128, 128], BF16, 1, sg) for i in range(4)]
            h3 = sb("h3", [128, D], F32, 1, sg)
            yt = h3
            aB = [sb(f"aB{i}", [128, 1], F32, 1, sg) for i in range(8)]
            xn3b = sb("xn3b", [128, D], BF16, 1, sg)
            n = 0
            for tt in range(8):
                xn_ = xn3 if tt % 2 == 0 else xn3b
                rms_ap(hres[:, tt, :], [hres.b[tt]], gffn, xn_[:], xn_.b, junk, ssq, rs)
                stash = {}
                for t in range(128 + 2):
                    if t < 128:
                        U = Ug[n % NG]; V = Vg[n % NGV]; a_ = aT[n % 8]; b_ = aB[n % 8]; g_ = ga[n % 8]; L_ = Lt[n % 4]
                        n += 1
                        stash[t] = (V, g_, L_)
                        dma("pool", lambda e: e.indirect_dma_start(out=U[:, :], out_offset=None, in_=pu16[:, :],
                                                                    in_offset=bass.IndirectOffsetOnAxis(ap=expT[:, tt, t:t + 1], axis=0)),
                            reads=[expT.b[tt]] + tabB[0:512], writes=U.b)
                        dma("pool", lambda e: e.indirect_dma_start(out=V[:, :], out_offset=None, in_=pv16[:, :],
                                                                    in_offset=bass.IndirectOffsetOnAxis(ap=expT[:, tt, t:t + 1], axis=0)),
                            reads=[expT.b[tt]] + tabB[512:1024], writes=V.b)
                        for hf, acc_ in ((0, a_), (1, b_)):
                            for nb in range(2):
                                bi = 4 + hf * 2 + nb
                                c0 = hf * 1024 + nb * 512
                                op("pe", lambda e: e.matmul(bank(bi), identb[:, t:t + 1].to_broadcast([128, 128]), xn_[:, c0:c0 + 512], start=True, stop=True),
                                   reads=CB + xn_.b, writes=[bankB[bi]])
                            op("dve", lambda e: e.scalar_tensor_tensor(out=junk[:, hf * 1024:(hf + 1) * 1024], in0=U[:, hf * 1024:(hf + 1) * 1024], scalar=1.0,
                                                                       in1=PB[:, hf * 1024:(hf + 1) * 1024], op0=ALU.mult, op1=ALU.mult, accum_out=acc_[:, 0:1]),
                               reads=U.b + bankB[4 + hf * 2:6 + hf * 2], writes=junk.b + acc_.b)
                        op("act", lambda e: e.activation(out=g_[:, 0:1], in_=a_[:, 0:1], func=AF.Gelu_apprx_tanh, bias=b_[:, 0:1]), reads=a_.b + b_.b, writes=g_.b)
                    t1_ = t - 1
                    if 0 <= t1_ < 128:
                        V1, g1, L1 = stash[t1_]
                        hd = hds[t1_ % 4]
                        op("act", lambda e: e.activation(out=hd[:, 0:1], in_=g1[:, 0:1], func=AF.Copy, scale=gateT[:, tt, t1_:t1_ + 1]), reads=g1.b + [gateT.b[tt]], writes=hd.b)
                        op("act", lambda e: e.activation(out=L1[:], in_=c255[:, 127 - t1_:255 - t1_], func=AF.Copy, scale=hd[:, 0:1]), reads=c255.b + hd.b, writes=L1.b)
                    t2_ = t - 2
                    if 0 <= t2_ < 128:
                        V2, g2, L2 = stash[t2_]
                        for nb in range(4):
                            op("pe", lambda e: e.matmul(bank(nb), L2[:], V2[:, nb * 512:(nb + 1) * 512], start=(t2_ == 0), stop=(t2_ == 127)),
                               reads=L2.b + V2.b, writes=[bankB[nb]])
                op("dve", lambda e: e.tensor_tensor(out=h3[:], in0=hres[:, tt, :], in1=PA[:, 0:2048], op=ALU.add), reads=[hres.b[tt]] + bankB[0:4], writes=h3.b)
                if stop == "h3":
                    outs.append(dma("sp", lambda e: e.dma_start(out=dbg_t[tt * 128:(tt + 1) * 128, :], in_=h3[:]), reads=h3.b))
                rms_ap(h3[:], h3.b, gfin, yt[:], yt.b, junk, ssq, rs)
                outs.append(dma("sp", lambda e: e.dma_start(out=y[tt * 128:(tt + 1) * 128, :], in_=yt[:]), reads=yt.b))
            sch.finish(outs)
    es.close()
    return nc


def make_in_maps(inputs):
    f32 = lambda a: np.ascontiguousarray(a, dtype=np.float32)
    sc = shared_consts()
    shared = {
        "t5": f32(inputs["t5_table"]),
        "g_mix": f32(inputs["norm_mix"][0][None]), "g_mq": f32(inputs["norm_mem_q"][0][None]),
        "g_mkv": f32(inputs["norm_mem_kv"][0][None]), "g_ffn": f32(inputs["norm_ffn"][0][None]),
        "g_fin": f32(inputs["norm_final"][None]),
        "w_in": f32(inputs["w_in"][0]),
        "posT": f32(np.stack([inputs["cmp_pos_k"][0].T, inputs["cmp_pos_v"][0].T])),
        "w1": f32(np.stack([inputs["cmp_w1_k"][0], inputs["cmp_w1_v"][0]])),
        "w2": f32(np.stack([inputs["cmp_w2_k"][0], inputs["cmp_w2_v"][0]])),
        "w_out": f32(inputs["w_out"][0]),
        "wmq": f32(inputs["w_mem_q"][0]), "wmk": f32(inputs["w_mem_k"][0]), "wmv": f32(inputs["w_mem_v"][0]),
        "wmo": f32(inputs["w_mem_o"][0]),
        "wpq": f32(inputs["peer_w_q"][0]),
        "skT": f32(np.transpose(inputs["peer_sub_keys"][0].reshape(16, 128, 128), (0, 2, 1))),
        "pu": f32(inputs["peer_u"][0]), "pv": f32(inputs["peer_v"][0]),
        "cmat": sc["cmat"], "esel": sc["esel"], "overlap": sc["overlap"], "c255": sc["c255"], "iota16": sc["iota16"],
    }
    pc = [host_consts(0), host_consts(1)]
    maps = []
    x = inputs["x"]; mem = inputs["mem"]
    for c in range(8):
        b, par = c // 2, c % 2
        q1, q2 = par, 3 - par
        own = np.concatenate([np.arange(512 * q1, 512 * q1 + 512), np.arange(512 * q2, 512 * q2 + 512)])
        m = dict(shared)
        m["xa"] = f32(x[b])
        m["xo"] = f32(x[b][own])
        m["memb"] = f32(mem[b])
        m.update(pc[par])
        maps.append(m)
    return maps


def own_rows(par):
    q1, q2 = par, 3 - par
    return np.concatenate([np.arange(512 * q1, 512 * q1 + 512), np.arange(512 * q2, 512 * q2 + 512)])


_NC_CACHE = {}


def kernel(**inputs):
    if "nc" not in _NC_CACHE:
        _NC_CACHE["nc"] = build_program("all")
    nc = _NC_CACHE["nc"]
    maps = make_in_maps(inputs)
    res = run_bass_kernel_spmd(nc, maps, core_ids=list(range(8)))
    out = np.zeros((4, S, D), np.float32)
    for c in range(8):
        b, par = c // 2, c % 2
        out[b, own_rows(par)] = res.results[c]["y"]
    return out
```
